# Optimizing a Trainium2 kernel written in Bass

```python
import math
import jax, jax.numpy as jnp
from jax import lax
import numpy as np

D_MODEL = 2048
BATCH = 1
SEQ = 16384
DEPTH = 1

MOBA_HEADS = 8
MOBA_HEAD_DIM = 128
MOBA_WIDTH = MOBA_HEADS * MOBA_HEAD_DIM
MOBA_BLOCK = 256
MOBA_TOPK = 3
MOBA_Q_CHUNK = 64
PARTIAL_ROPE_DIM = MOBA_HEAD_DIM // 4
MLA_HEADS = 8
MLA_NOPE_DIM = 128
MLA_ROPE_DIM = 64
MLA_V_DIM = 128
MLA_Q_RANK = 512
MLA_KV_RANK = 256
MLA_WIDTH = MLA_HEADS * MLA_V_DIM
MLA_Q_BLOCK = 128
ROPE_THETA = 500000.0
D_FF = -(-8 * D_MODEL // (3 * 256)) * 256
EPS = 1e-6
N_BRANCHES = 2
IN_SIZES = (MOBA_WIDTH, MOBA_WIDTH, MOBA_WIDTH, MLA_Q_RANK, MLA_KV_RANK, MLA_ROPE_DIM, N_BRANCHES * D_MODEL)
IN_COLS = sum(IN_SIZES)

kernel_name = "hybrid_moba_mla_gated_block"


def rmsnorm(x, g):
    x32 = x.astype(jnp.float32)
    y = x32 * lax.rsqrt(jnp.mean(x32 * x32, axis=-1, keepdims=True) + EPS)
    return (y * g.astype(jnp.float32)).astype(x.dtype)


def rope(x, positions):
    d = x.shape[-1]
    inv_freq = ROPE_THETA ** (-jnp.arange(0, d, 2, dtype=jnp.float32) / d)
    ang = positions.astype(jnp.float32)[..., None] * inv_freq
    cos = jnp.cos(ang)[:, :, None, :]
    sin = jnp.sin(ang)[:, :, None, :]
    x32 = x.astype(jnp.float32)
    x1, x2 = x32[..., : d // 2], x32[..., d // 2:]
    out = jnp.concatenate([x1 * cos - x2 * sin, x2 * cos + x1 * sin], axis=-1)
    return out.astype(x.dtype)


def moba_attention(q, k, v):
    B, S, H, Dh = q.shape
    nb = -(-S // MOBA_BLOCK)
    n_sel = min(MOBA_TOPK, nb)
    pad = nb * MOBA_BLOCK - S
    kp = jnp.pad(k, ((0, 0), (0, pad), (0, 0), (0, 0)))
    vp = jnp.pad(v, ((0, 0), (0, pad), (0, 0), (0, 0)))
    kb = kp.reshape(B, nb, MOBA_BLOCK, H, Dh).transpose(0, 3, 1, 2, 4)
    vb = vp.reshape(B, nb, MOBA_BLOCK, H, Dh).transpose(0, 3, 1, 2, 4)
    k_mean = jnp.mean(kb.astype(jnp.float32), axis=3)
    scale = Dh ** -0.5
    bidx = jnp.arange(B)[:, None, None, None]
    hidx = jnp.arange(H)[None, None, :, None]

    def chunk(c):
        start = c * MOBA_Q_CHUNK
        qc = lax.dynamic_slice_in_dim(q, start, MOBA_Q_CHUNK, axis=1)
        blk = start // MOBA_BLOCK
        qpos = start + jnp.arange(MOBA_Q_CHUNK)
        gate = jnp.einsum('bqhd,bhnd->bqhn', qc.astype(jnp.float32), k_mean)
        past = jnp.arange(nb) < blk
        gate = jnp.where(past[None, None, None, :], gate, -jnp.inf)
        _, sel = lax.top_k(gate, n_sel)
        sel_valid = jnp.arange(n_sel) < blk
        k_sel = kb[bidx, hidx, sel]
        v_sel = vb[bidx, hidx, sel]
        s_sel = jnp.einsum('bqhd,bqhkjd->bqhkj', qc, k_sel).astype(jnp.float32) * scale
        s_sel = jnp.where(sel_valid[None, None, None, :, None], s_sel, -jnp.inf)
        s_sel = s_sel.reshape(B, MOBA_Q_CHUNK, H, n_sel * MOBA_BLOCK)
        k_own = lax.dynamic_slice_in_dim(kp, blk * MOBA_BLOCK, MOBA_BLOCK, axis=1)
        v_own = lax.dynamic_slice_in_dim(vp, blk * MOBA_BLOCK, MOBA_BLOCK, axis=1)
        s_own = jnp.einsum('bqhd,bjhd->bqhj', qc, k_own).astype(jnp.float32) * scale
        kpos = blk * MOBA_BLOCK + jnp.arange(MOBA_BLOCK)
        causal = kpos[None, :] <= qpos[:, None]
        s_own = jnp.where(causal[None, :, None, :], s_own, -jnp.inf)
        p = jax.nn.softmax(jnp.concatenate([s_sel, s_own], axis=-1), axis=-1).astype(v.dtype)
        p_sel = p[..., : n_sel * MOBA_BLOCK].reshape(B, MOBA_Q_CHUNK, H, n_sel, MOBA_BLOCK)
        p_own = p[..., n_sel * MOBA_BLOCK:]
        return (jnp.einsum('bqhkj,bqhkjd->bqhd', p_sel, v_sel)
                + jnp.einsum('bqhj,bjhd->bqhd', p_own, v_own))

    outs = lax.map(chunk, jnp.arange(S // MOBA_Q_CHUNK))
    return outs.transpose(1, 0, 2, 3, 4).reshape(B, S, H, Dh)


def mla_attention(q_nope, q_rope, k_nope, k_rope, v):
    B, S, H, _ = q_nope.shape
    scale = (MLA_NOPE_DIM + MLA_ROPE_DIM) ** -0.5
    kpos = jnp.arange(S)

    def block(c):
        start = c * MLA_Q_BLOCK
        qn = lax.dynamic_slice_in_dim(q_nope, start, MLA_Q_BLOCK, axis=1)
        qr = lax.dynamic_slice_in_dim(q_rope, start, MLA_Q_BLOCK, axis=1)
        s = (jnp.einsum('bqhd,bkhd->bhqk', qn, k_nope)
             + jnp.einsum('bqhd,bkd->bhqk', qr, k_rope)).astype(jnp.float32) * scale
        qpos = start + jnp.arange(MLA_Q_BLOCK)
        s = jnp.where((kpos[None, :] <= qpos[:, None])[None, None], s, -jnp.inf)
        p = jax.nn.softmax(s, axis=-1).astype(v.dtype)
        return jnp.einsum('bhqk,bkhd->bqhd', p, v)

    outs = lax.map(block, jnp.arange(S // MLA_Q_BLOCK))
    return outs.transpose(1, 0, 2, 3, 4).reshape(B, S, H * MLA_V_DIM)


def setup_inputs(seed: int = 0) -> dict:
    key = jax.random.key(seed)
    ks = jax.random.split(key, 17)

    def dense(k, shape, fan_in):
        return jax.random.normal(k, shape, jnp.float32) * (fan_in ** -0.5)

    def gain(k, shape):
        return 1.0 + 0.02 * jax.random.normal(k, shape, jnp.float32)

    L = DEPTH
    return {
        "x": jax.random.normal(ks[0], (BATCH, SEQ, D_MODEL), jnp.float32),
        "positions": jnp.broadcast_to(jnp.arange(SEQ, dtype=jnp.int32), (BATCH, SEQ)),
        "attn_norm": gain(ks[1], (L, D_MODEL)),
        "w_in": dense(ks[2], (L, D_MODEL, IN_COLS), D_MODEL),
        "q_norm": gain(ks[3], (L, MLA_Q_RANK)),
        "w_uq": dense(ks[4], (L, MLA_Q_RANK, MLA_HEADS * (MLA_NOPE_DIM + MLA_ROPE_DIM)), MLA_Q_RANK),
        "kv_norm": gain(ks[5], (L, MLA_KV_RANK)),
        "w_ukv": dense(ks[6], (L, MLA_KV_RANK, MLA_HEADS * (MLA_NOPE_DIM + MLA_V_DIM)), MLA_KV_RANK),
        "w_branch_a": dense(ks[7], (L, MOBA_WIDTH, D_MODEL), MOBA_WIDTH),
        "w_branch_b": dense(ks[8], (L, MLA_WIDTH, D_MODEL), MLA_WIDTH),
        "w_out": dense(ks[9], (L, D_MODEL, D_MODEL), D_MODEL),
        "ffn_norm": gain(ks[10], (L, D_MODEL)),
        "w_gate": dense(ks[11], (L, D_MODEL, D_FF), D_MODEL),
        "w_up": dense(ks[12], (L, D_MODEL, D_FF), D_MODEL),
        "w_down": dense(ks[13], (L, D_FF, D_MODEL), D_FF),
        "final_norm": gain(ks[14], (D_MODEL,)),
    }


def reference(x, positions, attn_norm, w_in, q_norm, w_uq, kv_norm, w_ukv, w_branch_a, w_branch_b,
              w_out, ffn_norm, w_gate, w_up, w_down, final_norm):
    B, S, _ = x.shape
    split_at = np.cumsum(np.array(IN_SIZES))[:-1].tolist()
    h = x
    for l in range(DEPTH):
        xn = rmsnorm(h, attn_norm[l])
        proj = xn @ w_in[l]
        q_a, k_a, v_a, c_q, c_kv, k_r, gates = jnp.split(proj, split_at, axis=-1)

        q_a = q_a.reshape(B, S, MOBA_HEADS, MOBA_HEAD_DIM)
        k_a = k_a.reshape(B, S, MOBA_HEADS, MOBA_HEAD_DIM)
        v_a = v_a.reshape(B, S, MOBA_HEADS, MOBA_HEAD_DIM)
        q_a = jnp.concatenate([rope(q_a[..., :PARTIAL_ROPE_DIM], positions), q_a[..., PARTIAL_ROPE_DIM:]], axis=-1)
        k_a = jnp.concatenate([rope(k_a[..., :PARTIAL_ROPE_DIM], positions), k_a[..., PARTIAL_ROPE_DIM:]], axis=-1)
        y_a = moba_attention(q_a, k_a, v_a).reshape(B, S, MOBA_WIDTH)

        qh = (rmsnorm(c_q, q_norm[l]) @ w_uq[l]).reshape(B, S, MLA_HEADS, MLA_NOPE_DIM + MLA_ROPE_DIM)
        q_nope, q_rope = qh[..., :MLA_NOPE_DIM], rope(qh[..., MLA_NOPE_DIM:], positions)
        kv = (rmsnorm(c_kv, kv_norm[l]) @ w_ukv[l]).reshape(B, S, MLA_HEADS, MLA_NOPE_DIM + MLA_V_DIM)
        k_nope, v_b = kv[..., :MLA_NOPE_DIM], kv[..., MLA_NOPE_DIM:]
        k_rope = rope(k_r[:, :, None, :], positions)[:, :, 0, :]
        y_b = mla_attention(q_nope, q_rope, k_nope, k_rope, v_b)

        g = jax.nn.sigmoid(gates.astype(jnp.float32)).astype(x.dtype)
        g_a, g_b = g[..., :D_MODEL], g[..., D_MODEL:]
        mixed = g_a * (y_a @ w_branch_a[l]) + g_b * (y_b @ w_branch_b[l])
        h = h + mixed @ w_out[l]

        hn = rmsnorm(h, ffn_norm[l])
        h = h + (jax.nn.silu(hn @ w_gate[l]) * (hn @ w_up[l])) @ w_down[l]
    return rmsnorm(h, final_norm)
```

```python
import os
import numpy as np
import ml_dtypes
import concourse.bass as bass
import concourse.mybir as mybir
from concourse.bass_utils import run_bass_kernel_spmd

F32 = mybir.dt.float32
BF16 = mybir.dt.bfloat16
I32 = mybir.dt.int32
AF = mybir.ActivationFunctionType
ALU = mybir.AluOpType
AX = mybir.AxisListType

D = 2048
KC = 16
NCORE = 8
DFF = 5632
EPS = 1e-6
THETA = 500000.0
BIG = 30000.0
TWO_PI_INV = float(1.0 / (2.0 * np.pi))
SIN_SCALE = 6.28318


class Buf:
    __slots__ = ("name", "w", "rs", "sem", "cnt", "excl")

    def __init__(self, name, excl=False):
        self.name = name
        self.excl = excl
        self.w = None
        self.rs = []
        self.sem = None
        self.cnt = 0


class Sched:
    def __init__(self, nc):
        self.nc = nc
        self.names = ["pe", "act", "dve", "pool", "sp"]
        self.ops = {e: [] for e in self.names}
        self.sem = {e: nc.alloc_semaphore("s_" + e) for e in ("pe", "act", "dve", "pool")}
        self.cnt = {e: 0 for e in self.sem}
        self.known = {e: {} for e in self.names}
        self.pending = {e: [] for e in self.names}
        self.dma_bufs = []
        self.nsem = 0

    def _need(self, eng, toks):
        out = []
        kn = self.known[eng]
        for t in toks:
            if t is None:
                continue
            key, sem, val = t
            if key == eng and eng == "pe":
                continue
            if kn.get(key, 0) >= val:
                continue
            kn[key] = val
            out.append((sem, val))
        return out

    def _deps(self, eng, reads, writes):
        toks = []
        for b in reads:
            toks.append(b.w)
            if b.excl:
                toks.extend(b.rs)
        for b in writes:
            toks.append(b.w)
            toks.extend(b.rs)
        return self._need(eng, toks)

    def op(self, eng, fn, reads=(), writes=()):
        waits = self.pending[eng] + self._deps(eng, reads, writes)
        self.pending[eng] = []
        self.cnt[eng] += 1
        tok = (eng, self.sem[eng], self.cnt[eng])
        self.ops[eng].append((waits, fn, (self.sem[eng], 1)))
        for b in reads:
            b.rs.append(tok)
        for b in writes:
            b.w = tok
            b.rs = []

    def dma(self, q, out_ap, in_ap, reads=(), writes=(), chain=False, **kw):
        b = writes[0]
        saved = None
        if chain and b.w is not None and b.w[0] == "dma_" + b.name:
            saved = b.w
            b.w = None
        waits = self.pending[q] + self._deps(q, reads, writes)
        if saved is not None:
            b.w = saved
        self.pending[q] = []
        if b.sem is None:
            b.sem = self.nc.alloc_semaphore("d%d" % self.nsem)
            self.nsem += 1
            self.dma_bufs.append(b)
        b.cnt += 16
        tok = ("dma_" + b.name, b.sem, b.cnt)
        self.ops[q].append((waits, lambda e: e.dma_start(out=out_ap, in_=in_ap, **kw), (b.sem, 16)))
        for r in reads:
            r.rs.append(tok)
        for w in writes:
            w.w = tok
            w.rs = []

    def barrier(self):
        toks = [(e, self.sem[e], self.cnt[e]) for e in self.sem if self.cnt[e] > 0]
        toks += [("dma_" + b.name, b.sem, b.cnt) for b in self.dma_bufs]
        for e in self.names:
            self.pending[e] += self._need(e, toks)

    def finalize(self):
        self.barrier()
        nc = self.nc
        with nc.Block() as block:
            def mk(name):
                def body(e):
                    for waits, fn, inc in self.ops[name]:
                        for s, v in waits:
                            e.wait_ge(s, v)
                        ins = fn(e)
                        ins.then_inc(inc[0], inc[1])
                    for s, v in self.pending[name]:
                        e.wait_ge(s, v)
                return body
            block.tensor(mk("pe"))
            block.scalar(mk("act"))
            block.vector(mk("dve"))
            block.gpsimd(mk("pool"))
            block.sync(mk("sp"))


class Arena:
    def __init__(self, nc, base=16512, limit=229300):
        self.nc = nc
        self.off = base
        self.limit = limit
        self.n = 0

    def reset(self, off=0):
        self.off = off

    def alloc(self, shape, dtype, name=None):
        esz = 4 if dtype in (F32, I32) else 2
        nbytes = esz * int(np.prod(shape[1:]))
        self.off = (self.off + 31) // 32 * 32
        self.n += 1
        t = self.nc.alloc_sbuf_tensor_at("%s_%d" % (name or "t", self.n), list(shape), dtype, offset=self.off)
        self.off += nbytes
        assert self.off <= self.limit, ("SBUF overflow", name, self.off)
        return t.ap(), Buf("%s_%d" % (name or "t", self.n))


def build_program(S, stop=None, sub=None, small=False, debug=()):
    NT = S // 128
    NJ = NT // NCORE
    NB = 64
    NG = NJ // 4
    nc = bass.Bass("TRN2", target_bir_lowering=False)
    sc = Sched(nc)
    ar = Arena(nc)

    def din(name, shape, dt=F32):
        if small and name in (small if isinstance(small, tuple) else ("x_own", "wq", "wg", "w_uq", "wa", "wb", "wout", "w_gate", "w_up", "w_down")):
            return nc.dram_tensor(name, [128, 128], dt, kind="ExternalInput").ap()
        return nc.dram_tensor(name, list(shape), dt, kind="ExternalInput").ap()

    x_all = din("x_all", [S, D]); posT_all = din("posT_all", [128, NT], I32)
    x_own = din("x_own", [NJ * 128, D]); posT_own = din("posT_own", [128, NJ], I32)
    wk = din("wk", [D, 2368]); wq = din("wq", [D, 1536]); wg = din("wg", [D, 4096])
    w_uq = din("w_uq", [512, 1536]); w_ukv = din("w_ukv", [256, 2048])
    wa = din("wa", [1024, D]); wb = din("wb", [1024, D]); wout = din("wout", [D, D])
    w_gate = din("w_gate", [D, DFF]); w_up = din("w_up", [D, DFF]); w_down = din("w_down", [DFF, D])
    g_attn = din("g_attn", [1, D]); g_q = din("g_q", [1, 512]); g_kv = din("g_kv", [1, 256])
    g_ffn = din("g_ffn", [1, D]); g_fin = din("g_fin", [1, D])
    c_invA = din("c_invA", [128, 16]); c_invB = din("c_invB", [128, 32])
    c_ident = din("c_ident", [128, 128], BF16); c_E = din("c_E", [64, 64 * 128], BF16)
    c_dmask = din("c_dmask", [128, 8 * 128], BF16)
    c_pastv = din("c_pastv", [128, NJ * NB]); c_pastneg = din("c_pastneg", [128, NJ * NB])
    c_ownv = din("c_ownv", [128, NJ * NB])
    out_own = nc.dram_tensor("out_own", [NJ * 128, D], F32, kind="ExternalOutput").ap()

    def dscr(name, shape, dt=BF16):
        return nc.dram_tensor(name, list(shape), dt, kind="Internal").ap(), Buf(name)

    KaT_s, KaT_b = dscr("KaT_s", [8, 128, S]); KnT_s, KnT_b = dscr("KnT_s", [8, 128, S])
    KrT_s, KrT_b = dscr("KrT_s", [64, S])
    Va_s, Va_b = dscr("Va_s", [8, 128, NT, 129]); Vb_s, Vb_b = dscr("Vb_s", [8, 128, NT, 129])
    QaT_s, QaT_b = dscr("QaT_s", [128, 8, NJ, 128]); BT_s, BT_b = dscr("BT_s", [64, 8, NJ, 128])
    QnT_s, QnT_b = dscr("QnT_s", [128, 8, NJ, 128]); QrT_s, QrT_b = dscr("QrT_s", [64, 8, NJ, 128])
    y_s, y_b = dscr("y_s", [2, 8, 128, NJ * 128])
    wg_h, wg_hb = dscr("wg_h", [D, 4096]); wa_h, wa_hb = dscr("wa_h", [1024, D]); wb_h, wb_hb = dscr("wb_h", [1024, D])
    wout_h, wout_hb = dscr("wout_h", [D, D]); wgate_h, wgate_hb = dscr("wgate_h", [D, DFF]); wup_h, wup_hb = dscr("wup_h", [D, DFF])
    wdown_h, wdown_hb = dscr("wdown_h", [DFF, D])
    precast = []
    if not small:
        for (dst, db, src, R, C) in ((wg_h, wg_hb, wg, D, 4096), (wa_h, wa_hb, wa, 1024, D), (wb_h, wb_hb, wb, 1024, D), (wout_h, wout_hb, wout, D, D),
                                     (wgate_h, wgate_hb, w_gate, D, DFF), (wup_h, wup_hb, w_up, D, DFF), (wdown_h, wdown_hb, w_down, DFF, D)):
            for r0 in range(0, R, 2048):
                r1 = min(R, r0 + 2048)
                for c0 in range(0, C, 2048):
                    c1 = min(C, c0 + 2048)
                    precast.append((dst[r0:r1, c0:c1], src[r0:r1, c0:c1], db))

    def emit_precast(n):
        for _ in range(n):
            if precast:
                o_, i_, b_ = precast.pop(0)
                sc.dma("pool", o_, i_, writes=[b_], chain=True)
    out_b = Buf("out")
    scr = {"KaT_s": (KaT_s, KaT_b), "KnT_s": (KnT_s, KnT_b), "KrT_s": (KrT_s, KrT_b), "Va_s": (Va_s, Va_b), "Vb_s": (Vb_s, Vb_b),
           "QaT_s": (QaT_s, QaT_b), "BT_s": (BT_s, BT_b), "QnT_s": (QnT_s, QnT_b), "QrT_s": (QrT_s, QrT_b), "y_s": (y_s, y_b)}

    def finish():
        for name in debug:
            if name not in scr:
                continue
            a, b = scr[name]
            o = nc.dram_tensor("dbg_" + name, list(a.shape), BF16, kind="ExternalOutput").ap()
            sc.dma("sp", o, a, reads=[b], writes=[Buf("dbg_" + name)])
        sc.finalize()
        return nc

    ps = []
    for i in range(6):
        ps.append((nc.alloc_psum_tensor("ps%d" % i, [128, 512], F32).ap(), Buf("ps%d" % i, excl=True)))
    pt = []
    for i in range(2):
        pt.append((nc.alloc_psum_tensor("pt%d" % i, [128, 1024], BF16).ap(), Buf("pt%d" % i, excl=True)))
    rr = {"ps": 0, "pt": 0}

    def next_ps():
        rr["ps"] = (rr["ps"] + 1) % 6
        return ps[rr["ps"]]

    def next_pt():
        rr["pt"] = (rr["pt"] + 1) % 2
        return pt[rr["pt"]]

    ident, ident_b = ar.alloc([128, 128], BF16, "ident")
    invA, invA_b = ar.alloc([128, 16], F32, "invA")
    invB, invB_b = ar.alloc([128, 32], F32, "invB")
    kmean, kmean_b = ar.alloc([128, 8, NB], F32, "kmean")
    kmean_bf, kmean_bf_b = ar.alloc([128, 8, NB], BF16, "kmeanbf")
    eps_t, eps_b = ar.alloc([128, 1], F32, "eps")
    sc.dma("sp", ident, c_ident, writes=[ident_b])
    sc.dma("sp", invA, c_invA, writes=[invA_b])
    sc.dma("sp", invB, c_invB, writes=[invB_b])
    sc.op("dve", lambda e: e.memset(kmean, 0.0), writes=[kmean_b])
    sc.op("dve", lambda e: e.memset(eps_t, EPS), writes=[eps_b])
    PERSIST = ar.off

    def load_w(dst, dst_b, src, kc, c0, c1, chain=True):
        for k in range(kc):
            for cc in range(c0, c1, 2048):
                ce = min(c1, cc + 2048)
                sc.dma("pool", dst[:, k, cc - c0:ce - c0], src[k * 128:(k + 1) * 128, cc:ce], writes=[dst_b], chain=chain)

    def load_h(dst, dst_b, src_h, src_hb, kc, c0, c1):
        sc.dma("pool", dst[:, 0:kc, :], src_h.rearrange("(k p) n -> p k n", p=128)[:, :, c0:c1], reads=[src_hb], writes=[dst_b])

    def load_bcast(dst, dst_b, src, n):
        sc.dma("sp", dst, src.partition_broadcast(128)[:, 0, :], writes=[dst_b])

    def rmsnorm(src, src_b, n, gain, gain_b, dst, dst_b, tmp):
        (ss, ss_b), (rs, rs_b) = tmp
        sc.op("act", lambda e: e.activation(out=dst, in_=src, func=AF.Square, accum_out=ss),
              reads=[src_b], writes=[dst_b, ss_b])
        sc.op("act", lambda e: e.activation(out=rs, in_=ss, func=AF.Sqrt, bias=eps_t, scale=1.0 / n),
              reads=[ss_b, eps_b], writes=[rs_b])
        sc.op("dve", lambda e: e.reciprocal(out=rs, in_=rs), reads=[rs_b], writes=[rs_b])
        sc.op("dve", lambda e: e.scalar_tensor_tensor(out=dst, in0=src, scalar=rs, in1=gain, op0=ALU.mult, op1=ALU.mult),
              reads=[src_b, rs_b, gain_b], writes=[dst_b])

    def transposes(src, src_b, nblk, width, dst_fn, dst_b, evac="dve"):
        b0 = 0
        while b0 < nblk:
            nb = min(8, nblk - b0)
            (p, p_b) = next_pt()

            def f(e, b0=b0, nb=nb, p=p):
                ins = None
                for i in range(nb):
                    ins = e.transpose(p[:width, i * 128:(i + 1) * 128], src[:, (b0 + i) * width:(b0 + i + 1) * width], ident)
                return ins
            sc.op("pe", f, reads=[src_b, ident_b], writes=[p_b])
            d = dst_fn(b0, nb)
            pv = p[:width, :nb * 128].rearrange("p (b t) -> p b t", t=128)
            if evac == "dve":
                sc.op("dve", lambda e, d=d, pv=pv: e.tensor_copy(out=d, in_=pv), reads=[p_b], writes=[dst_b])
            else:
                sc.op("act", lambda e, d=d, pv=pv: e.copy(out=d, in_=pv), reads=[p_b], writes=[dst_b])
            b0 += nb

    def linear(xT, xT_b, nk, W, W_b, c0, ncol, tok0=0):
        (p, p_b) = next_ps()

        def f(e):
            ins = None
            for k in range(nk):
                ins = e.matmul(p[:, :ncol], lhsT=xT[:, k, tok0:tok0 + 128], rhs=W[:, k, c0:c0 + ncol],
                               start=(k == 0), stop=(k == nk - 1))
            return ins
        sc.op("pe", f, reads=[xT_b, W_b], writes=[p_b])
        return p, p_b

    def rope_tables(posf_col, posf_b, inv, inv_b, n, tmp):
        (ang, ang_b), (u, u_b), (ki, ki_b), (kf, kf_b), (g, g_b), (cs, cs_b) = tmp
        sc.op("dve", lambda e: e.tensor_scalar(out=ang[:, :n], in0=inv, scalar1=posf_col, scalar2=None, op0=ALU.mult),
              reads=[posf_b, inv_b], writes=[ang_b])
        sc.op("dve", lambda e: e.tensor_scalar(out=u[:, 0, :n], in0=ang[:, :n], scalar1=TWO_PI_INV, scalar2=0.25, op0=ALU.mult, op1=ALU.add),
              reads=[ang_b], writes=[u_b])
        sc.op("dve", lambda e: e.tensor_scalar(out=u[:, 1, :n], in0=ang[:, :n], scalar1=TWO_PI_INV, scalar2=None, op0=ALU.mult),
              reads=[ang_b, u_b], writes=[u_b])
        sc.op("dve", lambda e: e.tensor_copy(out=ki[:, :, :n], in_=u[:, :, :n]), reads=[u_b], writes=[ki_b])
        sc.op("dve", lambda e: e.tensor_copy(out=kf[:, :, :n], in_=ki[:, :, :n]), reads=[ki_b], writes=[kf_b])
        sc.op("dve", lambda e: e.tensor_tensor(out=u[:, :, :n], in0=u[:, :, :n], in1=kf[:, :, :n], op=ALU.subtract),
              reads=[u_b, kf_b], writes=[u_b])
        sc.op("dve", lambda e: e.tensor_single_scalar(out=g[:, :, :n], in_=u[:, :, :n], scalar=0.5, op=ALU.is_gt), reads=[u_b], writes=[g_b])
        sc.op("dve", lambda e: e.tensor_tensor(out=u[:, :, :n], in0=u[:, :, :n], in1=g[:, :, :n], op=ALU.subtract), reads=[u_b, g_b], writes=[u_b])
        sc.op("dve", lambda e: e.tensor_single_scalar(out=g[:, :, :n], in_=u[:, :, :n], scalar=-0.5, op=ALU.is_lt), reads=[u_b], writes=[g_b])
        sc.op("dve", lambda e: e.tensor_tensor(out=u[:, :, :n], in0=u[:, :, :n], in1=g[:, :, :n], op=ALU.add), reads=[u_b, g_b], writes=[u_b])
        sc.op("act", lambda e: e.activation(out=cs[:, :, :n], in_=u[:, :, :n], func=AF.Sin, scale=SIN_SCALE), reads=[u_b], writes=[cs_b])
        return cs, cs_b

    def rope_apply(src3, src_b, dst3, dst_b, H, r0, n, cs, cs_b, tmp):
        (t1, t1_b), (t2, t2_b) = tmp
        cosb = cs[:, 0, :n].unsqueeze(1).to_broadcast([128, H, n])
        sinb = cs[:, 1, :n].unsqueeze(1).to_broadcast([128, H, n])
        x1 = src3[:, :, r0:r0 + n]; x2 = src3[:, :, r0 + n:r0 + 2 * n]
        a1 = t1[:, :H * n].rearrange("p (h n) -> p h n", n=n); a2 = t2[:, :H * n].rearrange("p (h n) -> p h n", n=n)
        sc.op("dve", lambda e: e.tensor_tensor(out=a1, in0=x1, in1=cosb, op=ALU.mult), reads=[src_b, cs_b], writes=[t1_b])
        sc.op("dve", lambda e: e.tensor_tensor(out=a2, in0=x2, in1=sinb, op=ALU.mult), reads=[src_b, cs_b], writes=[t2_b])
        sc.op("dve", lambda e: e.tensor_tensor(out=dst3[:, :, r0:r0 + n], in0=a1, in1=a2, op=ALU.subtract), reads=[t1_b, t2_b], writes=[dst_b])
        sc.op("dve", lambda e: e.tensor_tensor(out=a1, in0=x2, in1=cosb, op=ALU.mult), reads=[src_b, cs_b], writes=[t1_b])
        sc.op("dve", lambda e: e.tensor_tensor(out=a2, in0=x1, in1=sinb, op=ALU.mult), reads=[src_b, cs_b], writes=[t2_b])
        sc.op("dve", lambda e: e.tensor_tensor(out=dst3[:, :, r0 + n:r0 + 2 * n], in0=a1, in1=a2, op=ALU.add), reads=[t1_b, t2_b], writes=[dst_b])

    def norm_tmps():
        return (ar.alloc([128, 1], F32, "ss"), ar.alloc([128, 1], F32, "rs"))

    def rope_tmps():
        return (ar.alloc([128, 32], F32, "ang"), ar.alloc([128, 2, 32], F32, "u"), ar.alloc([128, 2, 32], I32, "ki"),
                ar.alloc([128, 2, 32], F32, "kf"), ar.alloc([128, 2, 32], F32, "g"))

    def x_to_xnT(xsrc, row0, xs, xn, xnT, gA, gA_b, ntmp, q="sp"):
        sc.dma(q, xs[0], xsrc[row0:row0 + 128, :], writes=[xs[1]])
        rmsnorm(xs[0], xs[1], D, gA, gA_b, xn[0], xn[1], ntmp)
        transposes(xn[0], xn[1], 16, 128, lambda b0, nb: xnT[0][:, b0:b0 + nb, :], xnT[1])

    ar.reset(PERSIST)
    Wk, Wk_b = ar.alloc([128, 16, 2368], BF16, "Wk")
    Wukv, Wukv_b = ar.alloc([128, 2, 2048], BF16, "Wukv")
    gA, gA_b = ar.alloc([128, D], F32, "gA")
    gKV, gKV_b = ar.alloc([128, 256], F32, "gKV")
    posi, posi_b = ar.alloc([128, NT], I32, "posi")
    posf, posf_b = ar.alloc([128, NT], F32, "posf")
    load_w(Wk, Wk_b, wk, 16, 0, 2368)
    load_w(Wukv, Wukv_b, w_ukv, 2, 0, 2048)
    load_bcast(gA, gA_b, g_attn, D)
    load_bcast(gKV, gKV_b, g_kv, 256)
    sc.dma("sp", posi, posT_all, writes=[posi_b])
    sc.op("dve", lambda e, a=posf, b=posi: e.tensor_copy(out=a, in_=b), reads=[posi_b], writes=[posf_b])
    ntmpA = [norm_tmps() for _ in range(2)]
    ntmpC = norm_tmps()
    rtA = rope_tmps() + (ar.alloc([128, 2, 32], F32, "csA"),)
    rtB = rope_tmps() + (ar.alloc([128, 2, 32], F32, "csB"),)
    rt12 = (ar.alloc([128, 256], F32, "t1"), ar.alloc([128, 256], F32, "t2"))
    xs2 = [ar.alloc([128, D], F32, "xs") for _ in range(2)]
    xn2 = [ar.alloc([128, D], BF16, "xn") for _ in range(2)]
    xnT2 = [ar.alloc([128, 16, 128], BF16, "xnT") for _ in range(2)]
    ka_sb2 = [ar.alloc([128, 1024], BF16, "ka") for _ in range(2)]
    kn_sb2 = [ar.alloc([128, 1024], BF16, "kn")] * 2
    ckvn2 = [ar.alloc([128, 256], BF16, "ckvn") for _ in range(2)]
    ckvnT2 = [ar.alloc([128, 2, 128], BF16, "ckvnT") for _ in range(2)]
    kr_sb2 = [ar.alloc([128, 64], BF16, "kr") for _ in range(2)]
    kaT_st2 = [ar.alloc([128, 8, 512], BF16, "kaTst") for _ in range(2)]
    knT_st2 = [ar.alloc([128, 8, 512], BF16, "knTst") for _ in range(2)]
    krT_st2 = [ar.alloc([64, 512], BF16, "krTst") for _ in range(2)]
    va_st2 = [ar.alloc([128, 4, 8, 129], BF16, "vast") for _ in range(2)]
    vb_st2 = [ar.alloc([128, 4, 8, 129], BF16, "vbst") for _ in range(2)]
    ksum = ar.alloc([128, 8], F32, "ksum")
    for q_ in range(2):
        sc.op("pool", lambda e, q_=q_: e.memset(va_st2[q_][0], 1.0), writes=[va_st2[q_][1]])
        sc.op("pool", lambda e, q_=q_: e.memset(vb_st2[q_][0], 1.0), writes=[vb_st2[q_][1]])
    if stop == 0:
        return finish()

    def stageA0(i):
        xs = xs2[i % 2]; xn = xn2[i % 2]
        sc.dma("pool", xs[0], x_all[i * 128:(i + 1) * 128, :], writes=[xs[1]])
        rmsnorm(xs[0], xs[1], D, gA, gA_b, xn[0], xn[1], ntmpA[i % 2])

    def stageA(i):
        xn = xn2[i % 2]; xnT = xnT2[i % 2]
        tl = i % 4; G = i // 4
        ka_sb = ka_sb2[i % 2]; kr_sb = kr_sb2[i % 2]; ckvn = ckvn2[i % 2]; va_st = va_st2[G % 2]
        transposes(xn[0], xn[1], 16, 128, lambda b0, nb: xnT[0][:, b0:b0 + nb, :], xnT[1])
        csA, csA_b = rope_tables(posf[:, i:i + 1], posf_b, invA, invA_b, 16, rtA)
        csB, csB_b = rope_tables(posf[:, i:i + 1], posf_b, invB, invB_b, 32, rtB)
        p, p_b = linear(xnT[0], xnT[1], 16, Wk, Wk_b, 2048, 320)
        rmsnorm(p[:, 0:256], p_b, 256, gKV, gKV_b, ckvn[0], ckvn[1], ntmpC)
        rope_apply(p[:, 256:320].rearrange("p (h d) -> p h d", d=64), p_b, kr_sb[0].rearrange("p (h d) -> p h d", d=64), kr_sb[1], 1, 0, 32, csB, csB_b, rt12)
        for hg in range(2):
            p, p_b = linear(xnT[0], xnT[1], 16, Wk, Wk_b, hg * 512, 512)
            dst = ka_sb[0][:, hg * 512:(hg + 1) * 512]
            sc.op("act", lambda e, dst=dst, p=p: e.copy(out=dst, in_=p), reads=[p_b], writes=[ka_sb[1]])
            rope_apply(p.rearrange("p (h d) -> p h d", d=128), p_b, dst.rearrange("p (h d) -> p h d", d=128), ka_sb[1], 4, 0, 16, csA, csA_b, rt12)
        for hg in range(2):
            p, p_b = linear(xnT[0], xnT[1], 16, Wk, Wk_b, 1024 + hg * 512, 512)
            dst = va_st[0][:, tl, hg * 4:(hg + 1) * 4, 0:128]
            sc.op("act", lambda e, dst=dst, p=p: e.copy(out=dst, in_=p.rearrange("p (h d) -> p h d", d=128)), reads=[p_b], writes=[va_st[1]])

    def stageB1(i):
        tl = i % 4; G = i // 4
        ka_sb = ka_sb2[i % 2]; kr_sb = kr_sb2[i % 2]; ckvn = ckvn2[i % 2]; ckvnT = ckvnT2[i % 2]; kn_sb = kn_sb2[i % 2]
        kaT_st = kaT_st2[G % 2]; krT_st = krT_st2[G % 2]; vb_st = vb_st2[G % 2]
        transposes(ka_sb[0], ka_sb[1], 8, 128, lambda b0, nb: kaT_st[0][:, b0:b0 + nb, tl * 128:(tl + 1) * 128], kaT_st[1])
        sc.op("dve", lambda e, tl=tl, kaT_st=kaT_st: e.tensor_reduce(out=ksum[0], in_=kaT_st[0][:, :, tl * 128:(tl + 1) * 128], axis=AX.X, op=ALU.add),
              reads=[kaT_st[1]], writes=[ksum[1]])
        nblk = i // 2
        sc.op("dve", lambda e, nblk=nblk: e.tensor_tensor(out=kmean[:, :, nblk], in0=kmean[:, :, nblk], in1=ksum[0], op=ALU.add),
              reads=[ksum[1], kmean_b], writes=[kmean_b])
        transposes(kr_sb[0], kr_sb[1], 1, 64, lambda b0, nb: krT_st[0][:, tl * 128:(tl + 1) * 128].unsqueeze(1), krT_st[1])
        transposes(ckvn[0], ckvn[1], 2, 128, lambda b0, nb: ckvnT[0][:, b0:b0 + nb, :], ckvnT[1])
        for hg in range(2):
            p, p_b = linear(ckvnT[0], ckvnT[1], 2, Wukv, Wukv_b, hg * 512, 512)
            dst = kn_sb[0][:, hg * 512:(hg + 1) * 512]
            sc.op("act", lambda e, dst=dst, p=p: e.copy(out=dst, in_=p), reads=[p_b], writes=[kn_sb[1]])
        for hg in range(2):
            p, p_b = linear(ckvnT[0], ckvnT[1], 2, Wukv, Wukv_b, 1024 + hg * 512, 512)
            dst = vb_st[0][:, tl, hg * 4:(hg + 1) * 4, 0:128]
            sc.op("act", lambda e, dst=dst, p=p: e.copy(out=dst, in_=p.rearrange("p (h d) -> p h d", d=128)), reads=[p_b], writes=[vb_st[1]])

    def stageB2(i):
        tl = i % 4; G = i // 4
        kn_sb = kn_sb2[i % 2]
        kaT_st = kaT_st2[G % 2]; krT_st = krT_st2[G % 2]; vb_st = vb_st2[G % 2]; knT_st = knT_st2[G % 2]; va_st = va_st2[G % 2]
        transposes(kn_sb[0], kn_sb[1], 8, 128, lambda b0, nb: knT_st[0][:, b0:b0 + nb, tl * 128:(tl + 1) * 128], knT_st[1])
        if tl == 3:
            c0 = G * 512
            sc.dma("sp", KaT_s[:, :, c0:c0 + 512].rearrange("h d t -> d h t"), kaT_st[0], reads=[kaT_st[1]], writes=[KaT_b])
            sc.dma("sp", KnT_s[:, :, c0:c0 + 512].rearrange("h d t -> d h t"), knT_st[0], reads=[knT_st[1]], writes=[KnT_b])
            sc.dma("sp", KrT_s[:, c0:c0 + 512], krT_st[0], reads=[krT_st[1]], writes=[KrT_b])
            for tt in range(4):
                sc.dma("sp", Va_s[:, :, 4 * G + tt, :].rearrange("h p d -> p h d"), va_st[0][:, tt, :, :], reads=[va_st[1]], writes=[Va_b])
                sc.dma("sp", Vb_s[:, :, 4 * G + tt, :].rearrange("h p d -> p h d"), vb_st[0][:, tt, :, :], reads=[vb_st[1]], writes=[Vb_b])

    NTA = 4 if stop == 1 else NT
    stageA0(0)
    if NTA > 1:
        stageA0(1)
    stageA(0)
    for i in range(NTA):
        stageB1(i)
        if i + 1 < NTA:
            stageA(i + 1)
        if i + 2 < NTA:
            stageA0(i + 2)
        if i % 4 == 1:
            emit_precast(1)
        stageB2(i)
    emit_precast(len(precast))
    sc.op("dve", lambda e: e.tensor_scalar(out=kmean_bf, in0=kmean, scalar1=1.0 / 256.0, scalar2=None, op0=ALU.mult),
          reads=[kmean_b], writes=[kmean_bf_b])
    sc.barrier()
    if stop in (1, 2):
        return finish()

    ar.reset(PERSIST)
    Wq, Wq_b = ar.alloc([128, 16, 1536], BF16, "Wq")
    Wuq, Wuq_b = ar.alloc([128, 4, 1536], BF16, "Wuq")
    gA, gA_b = ar.alloc([128, D], F32, "gA")
    gQ, gQ_b = ar.alloc([128, 512], F32, "gQ")
    posi, posi_b = ar.alloc([128, NJ], I32, "posi")
    posf, posf_b = ar.alloc([128, NJ], F32, "posf")
    pastv, pastv_b = ar.alloc([128, NJ, NB], F32, "pastv")
    pastneg, pastneg_b = ar.alloc([128, NJ, NB], F32, "pastneg")
    ownv, ownv_b = ar.alloc([128, NJ, NB], F32, "ownv")
    load_w(Wq, Wq_b, wq, 16, 0, 1536)
    load_w(Wuq, Wuq_b, w_uq, 4, 0, 1536)
    load_bcast(gA, gA_b, g_attn, D)
    load_bcast(gQ, gQ_b, g_q, 512)
    sc.dma("sp", posi, posT_own, writes=[posi_b])
    sc.op("dve", lambda e, a=posf, b=posi: e.tensor_copy(out=a, in_=b), reads=[posi_b], writes=[posf_b])
    sc.dma("sp", pastv, c_pastv.rearrange("p (j n) -> p j n", n=NB), writes=[pastv_b])
    sc.dma("sp", pastneg, c_pastneg.rearrange("p (j n) -> p j n", n=NB), writes=[pastneg_b])
    sc.dma("sp", ownv, c_ownv.rearrange("p (j n) -> p j n", n=NB), writes=[ownv_b])
    ntmp = norm_tmps()
    rtA = rope_tmps() + (ar.alloc([128, 2, 32], F32, "csA"),)
    rtB = rope_tmps() + (ar.alloc([128, 2, 32], F32, "csB"),)
    rt12 = (ar.alloc([128, 256], F32, "t1"), ar.alloc([128, 256], F32, "t2"))
    xs2 = [ar.alloc([128, D], F32, "xs") for _ in range(2)]
    xn2 = [ar.alloc([128, D], BF16, "xn") for _ in range(2)]
    xnT2 = [ar.alloc([128, 16, 128], BF16, "xnT") for _ in range(2)]
    qa_sb = ar.alloc([128, 1024], BF16, "qa")
    qaT_st = ar.alloc([128, 8, 128], BF16, "qaTst")
    cqn = ar.alloc([128, 512], BF16, "cqn")
    cqnT = ar.alloc([128, 4, 128], BF16, "cqnT")
    qn_sb = ar.alloc([128, 1024], BF16, "qn")
    qr_sb = ar.alloc([128, 512], BF16, "qr")
    qnT_st = ar.alloc([128, 8, 128], BF16, "qnTst")
    qrT_st = ar.alloc([64, 8, 128], BF16, "qrTst")
    gate = ar.alloc([128, 8, NB], F32, "gate")
    top8 = ar.alloc([128, 8, 8], F32, "top8")
    sel = ar.alloc([128, 8, NB], F32, "sel")
    bias_bf = ar.alloc([128, 8 * NB], BF16, "biasbf")
    bT_st = ar.alloc([64, 8, 128], BF16, "bTst")

    for j in range(NJ):
        xs = xs2[j % 2]; xn = xn2[j % 2]; xnT = xnT2[j % 2]
        x_to_xnT(x_own, j * 128, xs, xn, xnT, gA, gA_b, ntmp)
        csA, csA_b = rope_tables(posf[:, j:j + 1], posf_b, invA, invA_b, 16, rtA)
        csB, csB_b = rope_tables(posf[:, j:j + 1], posf_b, invB, invB_b, 32, rtB)
        for hg in range(2):
            p, p_b = linear(xnT[0], xnT[1], 16, Wq, Wq_b, hg * 512, 512)
            dst = qa_sb[0][:, hg * 512:(hg + 1) * 512]
            sc.op("act", lambda e, dst=dst, p=p: e.copy(out=dst, in_=p), reads=[p_b], writes=[qa_sb[1]])
            rope_apply(p.rearrange("p (h d) -> p h d", d=128), p_b, dst.rearrange("p (h d) -> p h d", d=128), qa_sb[1], 4, 0, 16, csA, csA_b, rt12)
        transposes(qa_sb[0], qa_sb[1], 8, 128, lambda b0, nb: qaT_st[0][:, b0:b0 + nb, :], qaT_st[1])
        sc.dma("sp", QaT_s[:, :, j, :], qaT_st[0], reads=[qaT_st[1]], writes=[QaT_b])
        (pg, pg_b) = next_ps()

        def fg(e, pg=pg):
            ins = None
            for h in range(8):
                ins = e.matmul(pg[:, h * NB:(h + 1) * NB], lhsT=qaT_st[0][:, h, :], rhs=kmean_bf[:, h, :], start=True, stop=True)
            return ins
        sc.op("pe", fg, reads=[qaT_st[1], kmean_bf_b], writes=[pg_b])
        pvb = pastv[:, j, :].unsqueeze(1).to_broadcast([128, 8, NB])
        pnb = pastneg[:, j, :].unsqueeze(1).to_broadcast([128, 8, NB])
        owb = ownv[:, j, :].unsqueeze(1).to_broadcast([128, 8, NB])
        pg3 = pg.rearrange("p (h n) -> p h n", n=NB)
        sc.op("dve", lambda e, pg3=pg3, pvb=pvb: e.tensor_tensor(out=gate[0], in0=pg3, in1=pvb, op=ALU.mult), reads=[pg_b, pastv_b], writes=[gate[1]])
        sc.op("dve", lambda e, pnb=pnb: e.tensor_tensor(out=gate[0], in0=gate[0], in1=pnb, op=ALU.add), reads=[gate[1], pastneg_b], writes=[gate[1]])
        if "gate" in debug and j == 1:
            og = nc.dram_tensor("dbg_gate", [128, 8, NB], F32, kind="ExternalOutput").ap()
            sc.dma("sp", og, gate[0], reads=[gate[1]], writes=[Buf("dbg_gate")])
            ok = nc.dram_tensor("dbg_kmean", [128, 8, NB], F32, kind="ExternalOutput").ap()
            sc.dma("sp", ok, kmean, reads=[kmean_b], writes=[Buf("dbg_kmean")])
        for h in range(8):
            sc.op("dve", lambda e, h=h: e.max(out=top8[0][:, h, :], in_=gate[0][:, h, :]), reads=[gate[1]], writes=[top8[1]])
        for h in range(8):
            sc.op("dve", lambda e, h=h: e.tensor_scalar(out=sel[0][:, h, :], in0=gate[0][:, h, :], scalar1=top8[0][:, h, 2:3], scalar2=None, op0=ALU.is_ge),
                  reads=[gate[1], top8[1]], writes=[sel[1]])
        sc.op("dve", lambda e, pvb=pvb: e.tensor_tensor(out=sel[0], in0=sel[0], in1=pvb, op=ALU.mult), reads=[sel[1], pastv_b], writes=[sel[1]])
        sc.op("dve", lambda e, owb=owb: e.tensor_tensor(out=sel[0], in0=sel[0], in1=owb, op=ALU.add), reads=[sel[1], ownv_b], writes=[sel[1]])
        sc.op("dve", lambda e: e.tensor_scalar(out=bias_bf[0], in0=sel[0].rearrange("p h n -> p (h n)"), scalar1=-1.0, scalar2=BIG, op0=ALU.add, op1=ALU.mult),
              reads=[sel[1]], writes=[bias_bf[1]])
        transposes(bias_bf[0], bias_bf[1], 8, NB, lambda b0, nb: bT_st[0][:, b0:b0 + nb, :], bT_st[1])
        sc.dma("sp", BT_s[:, :, j, :], bT_st[0], reads=[bT_st[1]], writes=[BT_b])
        p, p_b = linear(xnT[0], xnT[1], 16, Wq, Wq_b, 1024, 512)
        rmsnorm(p, p_b, 512, gQ, gQ_b, cqn[0], cqn[1], ntmp)
        transposes(cqn[0], cqn[1], 4, 128, lambda b0, nb: cqnT[0][:, b0:b0 + nb, :], cqnT[1])
        for hg in range(2):
            p, p_b = linear(cqnT[0], cqnT[1], 4, Wuq, Wuq_b, hg * 512, 512)
            dst = qn_sb[0][:, hg * 512:(hg + 1) * 512]
            sc.op("act", lambda e, dst=dst, p=p: e.copy(out=dst, in_=p), reads=[p_b], writes=[qn_sb[1]])
        p, p_b = linear(cqnT[0], cqnT[1], 4, Wuq, Wuq_b, 1024, 512)
        rope_apply(p.rearrange("p (h d) -> p h d", d=64), p_b, qr_sb[0].rearrange("p (h d) -> p h d", d=64), qr_sb[1], 8, 0, 32, csB, csB_b, rt12)
        transposes(qn_sb[0], qn_sb[1], 8, 128, lambda b0, nb: qnT_st[0][:, b0:b0 + nb, :], qnT_st[1])
        transposes(qr_sb[0], qr_sb[1], 8, 64, lambda b0, nb: qrT_st[0][:, b0:b0 + nb, :], qrT_st[1])
        sc.dma("sp", QnT_s[:, :, j, :], qnT_st[0], reads=[qnT_st[1]], writes=[QnT_b])
        sc.dma("sp", QrT_s[:, :, j, :], qrT_st[0], reads=[qrT_st[1]], writes=[QrT_b])
    sc.barrier()
    if stop == 3:
        return finish()

    ar.reset(PERSIST)
    K1 = [ar.alloc([128, S], BF16, "K1") for _ in range(2)]
    V = [ar.alloc([128, NT, 129], BF16, "V") for _ in range(2)]
    K2 = ar.alloc([128, max(S, 8192)], BF16, "K2")
    Q1 = [ar.alloc([128, NJ, 128], BF16, "Q1") for _ in range(2)]
    Q2 = [ar.alloc([128, NJ, 128], BF16, "Q2") for _ in range(2)]
    sc.op("pool", lambda e: e.memset(K2[0][64:128, :], 0.0), writes=[K2[1]])
    for q_ in range(2):
        sc.op("pool", lambda e, q_=q_: e.memset(Q2[q_][0][64:128, :, :], 0.0), writes=[Q2[q_][1]])
    dmask, dmask_b = ar.alloc([128, 8, 128], BF16, "dmask")
    PT = [ar.alloc([128, 4, 128], BF16, "PT") for _ in range(4)]
    Pacc = [ar.alloc([128, 512], F32, "Pacc") for _ in range(2)]
    rinv = ar.alloc([128, 512], F32, "rinv")
    ones_f, ones_f_b = ar.alloc([128, 128], F32, "onesf")
    sc.op("pool", lambda e: e.memset(ones_f, 1.0), writes=[ones_f_b])
    y_st = [ar.alloc([128, 512], BF16, "yst") for _ in range(2)]
    sc.dma("sp", dmask, c_dmask.rearrange("p (m q) -> p m q", q=128), writes=[dmask_b])
    hidx = 0
    qcnt = [0]
    gcnt = [0]

    def load_head(idx):
        br_, h_ = idx // 8, idx % 8
        K_s, K_sb = (KaT_s, KaT_b) if br_ == 0 else (KnT_s, KnT_b)
        V_s, V_sb = (Va_s, Va_b) if br_ == 0 else (Vb_s, Vb_b)
        Q1_s, Q1_sb = (QaT_s, QaT_b) if br_ == 0 else (QnT_s, QnT_b)
        Q2_s, Q2_sb = (BT_s, BT_b) if br_ == 0 else (QrT_s, QrT_b)
        k1 = K1[idx % 2]; v = V[idx % 2]; q1 = Q1[idx % 2]; q2 = Q2[idx % 2]
        sc.dma("sp", q1[0], Q1_s[:, h_, :, :], reads=[Q1_sb], writes=[q1[1]])
        sc.dma("sp", q2[0][0:64, :, :], Q2_s[:, h_, :, :], reads=[Q2_sb], writes=[q2[1]])
        for c0 in range(0, S, 4096):
            c1 = min(S, c0 + 4096)
            sc.dma("sp", k1[0][:, c0:c1], K_s[h_, :, c0:c1], reads=[K_sb], writes=[k1[1]], chain=True)
        for t0 in range(0, NT, 32):
            t1 = min(NT, t0 + 32)
            sc.dma("sp", v[0][:, t0:t1, :], V_s[h_, :, t0:t1, :], reads=[V_sb], writes=[v[1]], chain=True)

    load_head(0)
    for br in range(2):
        if br == 0:
            sc.dma("sp", K2[0][0:64, :64 * 128], c_E, writes=[K2[1]])
            scale = 128.0 ** -0.5
        else:
            sc.dma("sp", K2[0][0:64, :S], KrT_s, reads=[KrT_b], writes=[K2[1]])
            scale = 192.0 ** -0.5
        for h in range(8):
            k1 = K1[hidx % 2]; v = V[hidx % 2]; q1 = Q1[hidx % 2]; q2 = Q2[hidx % 2]
            hidx += 1
            if hidx < 16:
                load_head(hidx)
            for gq in range(NJ // 4):
                nkt = 32 * gq + 32
                gidx = gcnt[0]; gcnt[0] += 1
                (po, po_b) = ps[gidx % 2]
                pacc = Pacc[gidx % 2]
                ys = y_st[gidx % 2]
                prev = None

                def emit_pv(args):
                    kt, c0, ptile = args

                    def fv(e, kt=kt, c0=c0, ptile=ptile, v=v, po=po, nkt=nkt):
                        pf = ptile[0].rearrange("p a b -> p (a b)")
                        return e.matmul(po[:, c0:512], lhsT=v[0][:, kt, 0:128], rhs=pf[:, c0:512], start=(kt == 0), stop=(kt == nkt - 1))
                    sc.op("pe", fv, reads=[ptile[1], v[1]], writes=[po_b])

                for kt in range(nkt):
                    rel = kt - 32 * gq
                    jmin = 0 if rel < 8 else rel // 8
                    c0 = jmin * 128
                    (pq, pq_b) = ps[2 + (qcnt[0] % 4)]
                    ptile = PT[qcnt[0] % 4]
                    qcnt[0] += 1

                    def fs(e, pq=pq, kt=kt, k1=k1, q1=q1, q2=q2, gq=gq, br=br, jmin=jmin, c0=c0):
                        r1 = q1[0][:, 4 * gq + jmin:4 * gq + 4, :].rearrange("p j q -> p (j q)")
                        r2 = q2[0][:, 4 * gq + jmin:4 * gq + 4, :].rearrange("p j q -> p (j q)")
                        e.matmul(pq[:, c0:512], lhsT=k1[0][:, kt * 128:(kt + 1) * 128], rhs=r1, start=True, stop=False)
                        if br == 0:
                            l2 = K2[0][:, (kt // 2) * 128:(kt // 2 + 1) * 128]
                        else:
                            l2 = K2[0][:, kt * 128:(kt + 1) * 128]
                        return e.matmul(pq[:, c0:512], lhsT=l2, rhs=r2, start=False, stop=True)
                    sc.op("pe", fs, reads=[k1[1], q1[1], q2[1], K2[1]], writes=[pq_b])
                    pflat = ptile[0].rearrange("p a b -> p (a b)")
                    sc.op("act", lambda e, pq=pq, pflat=pflat, scale=scale, c0=c0: e.activation(out=pflat[:, c0:512], in_=pq[:, c0:512], func=AF.Exp, scale=scale),
                          reads=[pq_b], writes=[ptile[1]])
                    if rel >= 0:
                        m = rel - 8 * jmin
                        sc.op("dve", lambda e, ptile=ptile, m=m, jmin=jmin: e.tensor_tensor(out=ptile[0][:, jmin, :], in0=ptile[0][:, jmin, :], in1=dmask[:, m, :], op=ALU.mult),
                              reads=[ptile[1], dmask_b], writes=[ptile[1]])
                    if kt == 0:
                        sc.op("dve", lambda e, pacc=pacc, pflat=pflat: e.tensor_copy(out=pacc[0], in_=pflat), reads=[ptile[1]], writes=[pacc[1]])
                    else:
                        sc.op("dve", lambda e, pacc=pacc, pflat=pflat, c0=c0: e.tensor_tensor(out=pacc[0][:, c0:512], in0=pacc[0][:, c0:512], in1=pflat[:, c0:512], op=ALU.add),
                              reads=[ptile[1], pacc[1]], writes=[pacc[1]])
                    if prev is not None:
                        emit_pv(prev)
                    prev = (kt, c0, ptile)
                emit_pv(prev)
                (pr, pr_b) = ps[2 + (qcnt[0] % 4)]
                qcnt[0] += 1
                sc.op("pe", lambda e, pr=pr, pacc=pacc: e.matmul(pr, lhsT=ones_f, rhs=pacc[0], start=True, stop=True), reads=[pacc[1], ones_f_b], writes=[pr_b])
                sc.op("dve", lambda e, pr=pr: e.reciprocal(out=rinv[0], in_=pr), reads=[pr_b], writes=[rinv[1]])
                sc.op("dve", lambda e, po=po, ys=ys: e.tensor_tensor(out=ys[0], in0=po, in1=rinv[0], op=ALU.mult), reads=[po_b, rinv[1]], writes=[ys[1]])
                sc.dma("sp", y_s[br, h, :, gq * 512:(gq + 1) * 512], ys[0], reads=[ys[1]], writes=[y_b])
    sc.barrier()
    if stop == 4:
        return finish()

    ar.reset(PERSIST)
    if os.environ.get("XPAD", "0") == "1":
        ar.alloc([128, 2048], BF16, "pad")
    ntmp = norm_tmps()
    h1 = [ar.alloc([128, D], F32, "h1") for _ in range(4)]
    xn = ar.alloc([128, D], BF16, "xn")
    xnT = ar.alloc([128, 16, 512], BF16, "xnT")
    Wc = [ar.alloc([128, 16, 512], BF16, "Wc") for _ in range(3)]
    CBASE = ar.off
    gA, gA_b = ar.alloc([128, D], F32, "gA")
    yin = [ar.alloc([128, 1024], BF16, "yin") for _ in range(2)]
    yT = [ar.alloc([128, 8, 512], BF16, "yT") for _ in range(2)]
    gates = ar.alloc([128, 4, 4096], BF16, "gates")
    mixed = ar.alloc([128, 4, D], BF16, "mixed")
    mixedT = ar.alloc([128, 16, 512], BF16, "mixedT")
    tmpf = ar.alloc([128, 512], F32, "tmpf")
    ar.reset(CBASE)
    gF, gF_b = ar.alloc([128, D], F32, "gF")
    gO, gO_b = ar.alloc([128, D], F32, "gO")
    actc = [ar.alloc([128, 512], BF16, "actc") for _ in range(2)]
    actT = ar.alloc([128, 44, 512], BF16, "actT")
    sg = ar.alloc([128, 512], F32, "sg")
    Wd = [ar.alloc([128, 11, 512], BF16, "Wd") for _ in range(2)]
    wci = [0]

    def next_w():
        wci[0] = (wci[0] + 1) % 3
        return Wc[wci[0]]

    for G in range(NG):
        load_bcast(gA, gA_b, g_attn, D)
        for t in range(4):
            row0 = (G * 4 + t) * 128
            sc.dma("sp", h1[t][0], x_own[row0:row0 + 128, :], writes=[h1[t][1]])
            rmsnorm(h1[t][0], h1[t][1], D, gA, gA_b, xn[0], xn[1], ntmp)
            transposes(xn[0], xn[1], 16, 128, lambda b0, nb, t=t: xnT[0][:, b0:b0 + nb, t * 128:(t + 1) * 128], xnT[1])
        for br in range(2):
            sc.dma("sp", yT[br][0], y_s[br, :, :, G * 512:(G + 1) * 512].rearrange("h d q -> d h q"), reads=[y_b], writes=[yT[br][1]])
        for cg in range(8):
            w = next_w()
            load_h(w[0], w[1], wg_h, wg_hb, 16, cg * 512, (cg + 1) * 512)
            for t in range(4):
                p, p_b = linear(xnT[0], xnT[1], 16, w[0], w[1], 0, 512, tok0=t * 128)
                sc.op("act", lambda e, p=p, t=t, cg=cg: e.activation(out=gates[0][:, t, cg * 512:(cg + 1) * 512], in_=p, func=AF.Sigmoid),
                      reads=[p_b], writes=[gates[1]])
        for cg in range(4):
            wA = next_w()
            load_h(wA[0][:, 0:8, :], wA[1], wa_h, wa_hb, 8, cg * 512, (cg + 1) * 512)
            load_h(wA[0][:, 8:16, :], wA[1], wb_h, wb_hb, 8, cg * 512, (cg + 1) * 512)
            for t in range(4):
                pa, pa_b = linear(yT[0][0], yT[0][1], 8, wA[0], wA[1], 0, 512, tok0=t * 128)
                (pb, pb_b) = next_ps()

                def fb(e, pb=pb, wA=wA, t=t):
                    ins = None
                    for k in range(8):
                        ins = e.matmul(pb, lhsT=yT[1][0][:, k, t * 128:(t + 1) * 128], rhs=wA[0][:, 8 + k, :], start=(k == 0), stop=(k == 7))
                    return ins
                sc.op("pe", fb, reads=[yT[1][1], wA[1]], writes=[pb_b])
                sc.op("dve", lambda e, pa=pa, t=t, cg=cg: e.tensor_tensor(out=tmpf[0], in0=pa, in1=gates[0][:, t, cg * 512:(cg + 1) * 512], op=ALU.mult),
                      reads=[pa_b, gates[1]], writes=[tmpf[1]])
                sc.op("dve", lambda e, pb=pb, t=t, cg=cg: e.tensor_tensor(out=pb, in0=pb, in1=gates[0][:, t, 2048 + cg * 512:2048 + (cg + 1) * 512], op=ALU.mult),
                      reads=[pb_b, gates[1]], writes=[pb_b])
                sc.op("dve", lambda e, pb=pb, t=t, cg=cg: e.tensor_tensor(out=mixed[0][:, t, cg * 512:(cg + 1) * 512], in0=pb, in1=tmpf[0], op=ALU.add),
                      reads=[pb_b, tmpf[1]], writes=[mixed[1]])
        for t in range(4):
            transposes(mixed[0][:, t, :], mixed[1], 16, 128, lambda b0, nb, t=t: mixedT[0][:, b0:b0 + nb, t * 128:(t + 1) * 128], mixedT[1])
        for cg in range(4):
            w = next_w()
            load_h(w[0], w[1], wout_h, wout_hb, 16, cg * 512, (cg + 1) * 512)
            for t in range(4):
                p, p_b = linear(mixedT[0], mixedT[1], 16, w[0], w[1], 0, 512, tok0=t * 128)
                sc.op("dve", lambda e, p=p, t=t, cg=cg: e.tensor_tensor(out=h1[t][0][:, cg * 512:(cg + 1) * 512], in0=p, in1=h1[t][0][:, cg * 512:(cg + 1) * 512], op=ALU.add),
                      reads=[p_b, h1[t][1]], writes=[h1[t][1]])
        sc.barrier()
        if stop == 5:
            return finish()
        load_bcast(gF, gF_b, g_ffn, D); load_bcast(gO, gO_b, g_fin, D)
        hnT = xnT
        for t in range(4):
            rmsnorm(h1[t][0], h1[t][1], D, gF, gF_b, xn[0], xn[1], ntmp)
            transposes(xn[0], xn[1], 16, 128, lambda b0, nb, t=t: hnT[0][:, b0:b0 + nb, t * 128:(t + 1) * 128], hnT[1])
        ai = 0
        for cg in range(11):
            wG = next_w(); wU = next_w()
            load_h(wG[0], wG[1], wgate_h, wgate_hb, 16, cg * 512, (cg + 1) * 512)
            load_h(wU[0], wU[1], wup_h, wup_hb, 16, cg * 512, (cg + 1) * 512)
            for t in range(4):
                pgt, pgt_b = linear(hnT[0], hnT[1], 16, wG[0], wG[1], 0, 512, tok0=t * 128)
                put, put_b = linear(hnT[0], hnT[1], 16, wU[0], wU[1], 0, 512, tok0=t * 128)
                ac = actc[ai % 2]; ai += 1
                sc.op("act", lambda e, pgt=pgt: e.activation(out=sg[0], in_=pgt, func=AF.Silu), reads=[pgt_b], writes=[sg[1]])
                sc.op("dve", lambda e, put=put, ac=ac: e.tensor_tensor(out=ac[0], in0=put, in1=sg[0], op=ALU.mult),
                      reads=[put_b, sg[1]], writes=[ac[1]])
                transposes(ac[0], ac[1], 4, 128, lambda b0, nb, t=t, cg=cg: actT[0][:, cg * 4 + b0:cg * 4 + b0 + nb, t * 128:(t + 1) * 128], actT[1], evac="act")
        if stop == 6:
            return finish()
        for cg in range(4):
            accs = [next_ps() for _ in range(4)]
            for fq in range(4):
                wd = Wd[(cg * 4 + fq) % 2]
                load_h(wd[0], wd[1], wdown_h[fq * 1408:(fq + 1) * 1408, :], wdown_hb, 11, cg * 512, (cg + 1) * 512)
                for t in range(4):
                    def fd(e, acc=accs[t][0], wd=wd, t=t, fq=fq):
                        ins = None
                        for f in range(11):
                            fc = fq * 11 + f
                            ins = e.matmul(acc, lhsT=actT[0][:, fc, t * 128:(t + 1) * 128], rhs=wd[0][:, f, :], start=(fc == 0), stop=(fc == 43))
                        return ins
                    sc.op("pe", fd, reads=[actT[1], wd[1]], writes=[accs[t][1]])
            XPC = int(os.environ.get("XPC", "0"))
            for t in range(4):
                if XPC == 1:
                    break
                sc.op("dve", lambda e, t=t, cg=cg, acc=accs[t][0]: e.tensor_tensor(out=h1[t][0][:, cg * 512:(cg + 1) * 512], in0=acc, in1=h1[t][0][:, cg * 512:(cg + 1) * 512], op=ALU.add),
                      reads=[accs[t][1], h1[t][1]], writes=[h1[t][1]])
            if XPC == 2:
                break
        if stop == 7:
            return finish()
        for t in range(4):
            row0 = (G * 4 + t) * 128
            (ss, ss_b), (rs, rs_b) = ntmp
            sc.op("act", lambda e, t=t: e.activation(out=xn[0], in_=h1[t][0], func=AF.Square, accum_out=ss), reads=[h1[t][1]], writes=[xn[1], ss_b])
            sc.op("act", lambda e: e.activation(out=rs, in_=ss, func=AF.Sqrt, bias=eps_t, scale=1.0 / D), reads=[ss_b, eps_b], writes=[rs_b])
            sc.op("dve", lambda e: e.reciprocal(out=rs, in_=rs), reads=[rs_b], writes=[rs_b])
            sc.op("dve", lambda e, t=t: e.scalar_tensor_tensor(out=h1[t][0], in0=h1[t][0], scalar=rs, in1=gO, op0=ALU.mult, op1=ALU.mult),
                  reads=[h1[t][1], rs_b, gO_b], writes=[h1[t][1]])
            sc.dma("sp", out_own[row0:row0 + 128, :], h1[t][0], reads=[h1[t][1]], writes=[out_b])
        sc.barrier()

    return finish()


def host_inputs(S, x, positions, attn_norm, w_in, q_norm, w_uq, kv_norm, w_ukv, w_branch_a, w_branch_b,
                w_out, ffn_norm, w_gate, w_up, w_down, final_norm):
    NT = S // 128
    NJ = NT // NCORE
    NB = 64
    f = lambda a: np.ascontiguousarray(np.asarray(a, dtype=np.float32))
    x2 = f(x).reshape(S, D)
    pos = np.asarray(positions).reshape(S).astype(np.int32)
    w_in0 = f(w_in)[0]
    common = {
        "x_all": x2,
        "posT_all": np.ascontiguousarray(pos.reshape(NT, 128).T),
        "wk": np.ascontiguousarray(np.concatenate([w_in0[:, 1024:2048], w_in0[:, 2048:3072], w_in0[:, 3584:3840], w_in0[:, 3840:3904]], axis=1)),
        "wq": np.ascontiguousarray(np.concatenate([w_in0[:, 0:1024], w_in0[:, 3072:3584]], axis=1)),
        "wg": np.ascontiguousarray(w_in0[:, 3904:8000]),
        "wa": f(w_branch_a)[0], "wb": f(w_branch_b)[0], "wout": f(w_out)[0],
        "w_gate": f(w_gate)[0], "w_up": f(w_up)[0], "w_down": f(w_down)[0],
        "g_attn": f(attn_norm).reshape(1, D), "g_q": f(q_norm).reshape(1, 512), "g_kv": f(kv_norm).reshape(1, 256),
        "g_ffn": f(ffn_norm).reshape(1, D), "g_fin": f(final_norm).reshape(1, D),
    }
    uq = f(w_uq)[0].reshape(512, 8, 192)
    common["w_uq"] = np.ascontiguousarray(np.concatenate([uq[:, :, :128].reshape(512, 1024), uq[:, :, 128:].reshape(512, 512)], axis=1))
    ukv = f(w_ukv)[0].reshape(256, 8, 256)
    common["w_ukv"] = np.ascontiguousarray(np.concatenate([ukv[:, :, :128].reshape(256, 1024), ukv[:, :, 128:].reshape(256, 1024)], axis=1))
    invA = (THETA ** (-np.arange(0, 32, 2, dtype=np.float32) / np.float32(32))).astype(np.float32)
    invB = (THETA ** (-np.arange(0, 64, 2, dtype=np.float32) / np.float32(64))).astype(np.float32)
    common["c_invA"] = np.ascontiguousarray(np.broadcast_to(invA, (128, 16)))
    common["c_invB"] = np.ascontiguousarray(np.broadcast_to(invB, (128, 32)))
    common["c_ident"] = np.eye(128, dtype=np.float32).astype(ml_dtypes.bfloat16)
    E = np.zeros((64, 64, 128), np.float32)
    for n in range(64):
        E[n, n, :] = 1.0
    common["c_E"] = E.reshape(64, 64 * 128).astype(ml_dtypes.bfloat16)
    maps = []
    kk = np.arange(128)[:, None]
    qq = np.arange(128)[None, :]
    for c in range(NCORE):
        m = dict(common)
        tiles = [8 * j + c for j in range(NJ)]
        m["x_own"] = np.ascontiguousarray(np.concatenate([x2[g * 128:(g + 1) * 128] for g in tiles], axis=0))
        m["posT_own"] = np.ascontiguousarray(np.stack([pos[g * 128:(g + 1) * 128] for g in tiles], axis=1))
        dm = np.zeros((128, 8, 128), np.float32)
        for mm in range(8):
            if mm < c:
                dm[:, mm, :] = 1.0
            elif mm == c:
                dm[:, mm, :] = (kk <= qq).astype(np.float32)
        m["c_dmask"] = dm.reshape(128, 8 * 128).astype(ml_dtypes.bfloat16)
        pv = np.zeros((NJ, NB), np.float32); ov = np.zeros((NJ, NB), np.float32)
        for j, g in enumerate(tiles):
            b = g // 2
            pv[j, :b] = 1.0
            ov[j, b] = 1.0
        pn = (pv - 1.0) * BIG
        m["c_pastv"] = np.ascontiguousarray(np.broadcast_to(pv.reshape(1, -1), (128, NJ * NB)))
        m["c_pastneg"] = np.ascontiguousarray(np.broadcast_to(pn.reshape(1, -1), (128, NJ * NB)))
        m["c_ownv"] = np.ascontiguousarray(np.broadcast_to(ov.reshape(1, -1), (128, NJ * NB)))
        maps.append(m)
    return maps


_CACHE = {}


def run(S, **inputs):
    if S not in _CACHE:
        _CACHE[S] = build_program(S)
    nc = _CACHE[S]
    maps = host_inputs(S, **inputs)
    import os
    ncr = int(os.environ.get('KDEBUG_CORES', NCORE))
    res = run_bass_kernel_spmd(nc, maps[:ncr], core_ids=list(range(ncr)))
    NT = S // 128
    NJ = NT // NCORE
    out = np.zeros((S, D), np.float32)
    for c in range(ncr):
        o = res.results[c]["out_own"]
        for j in range(NJ):
            g = 8 * j + c
            out[g * 128:(g + 1) * 128] = o[j * 128:(j + 1) * 128]
    return out.reshape(1, S, D)


def kernel(**inputs):
    S = int(np.asarray(inputs["x"]).shape[1])
    return run(S, **inputs)
```

```python
import os
import numpy as np
import ml_dtypes
import concourse.bass as bass
import concourse.mybir as mybir
from concourse.bass_utils import run_bass_kernel_spmd

F32 = mybir.dt.float32
BF16 = mybir.dt.bfloat16
I32 = mybir.dt.int32
AF = mybir.ActivationFunctionType
ALU = mybir.AluOpType
AX = mybir.AxisListType

D = 2048
KC = 16
NCORE = 8
DFF = 5632
EPS = 1e-6
THETA = 500000.0
BIG = 30000.0
TWO_PI_INV = float(1.0 / (2.0 * np.pi))
SIN_SCALE = 6.28318


class Buf:
    __slots__ = ("name", "w", "rs", "sem", "cnt", "excl")

    def __init__(self, name, excl=False):
        self.name = name
        self.excl = excl
        self.w = None
        self.rs = []
        self.sem = None
        self.cnt = 0


class Sched:
    def __init__(self, nc):
        self.nc = nc
        self.names = ["pe", "act", "dve", "pool", "sp"]
        self.ops = {e: [] for e in self.names}
        self.sem = {e: nc.alloc_semaphore("s_" + e) for e in ("pe", "act", "dve", "pool")}
        self.cnt = {e: 0 for e in self.sem}
        self.known = {e: {} for e in self.names}
        self.pending = {e: [] for e in self.names}
        self.dma_bufs = []
        self.nsem = 0

    def _need(self, eng, toks):
        out = []
        kn = self.known[eng]
        for t in toks:
            if t is None:
                continue
            key, sem, val = t
            if key == eng and eng == "pe":
                continue
            if kn.get(key, 0) >= val:
                continue
            kn[key] = val
            out.append((sem, val))
        return out

    def _deps(self, eng, reads, writes):
        toks = []
        for b in reads:
            toks.append(b.w)
            if b.excl:
                toks.extend(b.rs)
        for b in writes:
            toks.append(b.w)
            toks.extend(b.rs)
        return self._need(eng, toks)

    def op(self, eng, fn, reads=(), writes=()):
        waits = self.pending[eng] + self._deps(eng, reads, writes)
        self.pending[eng] = []
        self.cnt[eng] += 1
        tok = (eng, self.sem[eng], self.cnt[eng])
        self.ops[eng].append((waits, fn, (self.sem[eng], 1)))
        for b in reads:
            b.rs.append(tok)
        for b in writes:
            b.w = tok
            b.rs = []

    def dma(self, q, out_ap, in_ap, reads=(), writes=(), chain=False, **kw):
        b = writes[0]
        saved = None
        if chain and b.w is not None and b.w[0] == "dma_" + b.name:
            saved = b.w
            b.w = None
        waits = self.pending[q] + self._deps(q, reads, writes)
        if saved is not None:
            b.w = saved
        self.pending[q] = []
        if b.sem is None:
            b.sem = self.nc.alloc_semaphore("d%d" % self.nsem)
            self.nsem += 1
            self.dma_bufs.append(b)
        b.cnt += 16
        tok = ("dma_" + b.name, b.sem, b.cnt)
        self.ops[q].append((waits, lambda e: e.dma_start(out=out_ap, in_=in_ap, **kw), (b.sem, 16)))
        for r in reads:
            r.rs.append(tok)
        for w in writes:
            w.w = tok
            w.rs = []

    def barrier(self):
        toks = [(e, self.sem[e], self.cnt[e]) for e in self.sem if self.cnt[e] > 0]
        toks += [("dma_" + b.name, b.sem, b.cnt) for b in self.dma_bufs]
        for e in self.names:
            self.pending[e] += self._need(e, toks)

    def finalize(self):
        self.barrier()
        nc = self.nc
        with nc.Block() as block:
            def mk(name):
                def body(e):
                    for waits, fn, inc in self.ops[name]:
                        for s, v in waits:
                            e.wait_ge(s, v)
                        ins = fn(e)
                        ins.then_inc(inc[0], inc[1])
                    for s, v in self.pending[name]:
                        e.wait_ge(s, v)
                return body
            block.tensor(mk("pe"))
            block.scalar(mk("act"))
            block.vector(mk("dve"))
            block.gpsimd(mk("pool"))
            block.sync(mk("sp"))


class Arena:
    def __init__(self, nc, base=16512, limit=229300):
        self.nc = nc
        self.off = base
        self.limit = limit
        self.n = 0

    def reset(self, off=0):
        self.off = off

    def alloc(self, shape, dtype, name=None):
        esz = 4 if dtype in (F32, I32) else 2
        nbytes = esz * int(np.prod(shape[1:]))
        self.off = (self.off + 31) // 32 * 32
        self.n += 1
        t = self.nc.alloc_sbuf_tensor_at("%s_%d" % (name or "t", self.n), list(shape), dtype, offset=self.off)
        self.off += nbytes
        assert self.off <= self.limit, ("SBUF overflow", name, self.off)
        return t.ap(), Buf("%s_%d" % (name or "t", self.n))


def build_program(S, stop=None, sub=None, small=False, debug=()):
    NT = S // 128
    NJ = NT // NCORE
    NB = 64
    NG = NJ // 4
    nc = bass.Bass("TRN2", target_bir_lowering=False)
    sc = Sched(nc)
    ar = Arena(nc)

    def din(name, shape, dt=F32):
        if small and name in (small if isinstance(small, tuple) else ("x_own", "wq", "wg", "w_uq", "wa", "wb", "wout", "w_gate", "w_up", "w_down")):
            return nc.dram_tensor(name, [128, 128], dt, kind="ExternalInput").ap()
        return nc.dram_tensor(name, list(shape), dt, kind="ExternalInput").ap()

    x_all = din("x_all", [S, D]); posT_all = din("posT_all", [128, NT], I32)
    x_own = din("x_own", [NJ * 128, D]); posT_own = din("posT_own", [128, NJ], I32)
    wk = din("wk", [D, 2368]); wq = din("wq", [D, 1536]); wg = din("wg", [D, 4096])
    w_uq = din("w_uq", [512, 1536]); w_ukv = din("w_ukv", [256, 2048])
    wa = din("wa", [1024, D]); wb = din("wb", [1024, D]); wout = din("wout", [D, D])
    w_gate = din("w_gate", [D, DFF]); w_up = din("w_up", [D, DFF]); w_down = din("w_down", [DFF, D])
    g_attn = din("g_attn", [1, D]); g_q = din("g_q", [1, 512]); g_kv = din("g_kv", [1, 256])
    g_ffn = din("g_ffn", [1, D]); g_fin = din("g_fin", [1, D])
    c_invA = din("c_invA", [128, 16]); c_invB = din("c_invB", [128, 32])
    c_ident = din("c_ident", [128, 128], BF16); c_E = din("c_E", [64, 64 * 128], BF16)
    c_dmask = din("c_dmask", [128, 8 * 128], BF16)
    c_pastv = din("c_pastv", [128, NJ * NB]); c_pastneg = din("c_pastneg", [128, NJ * NB])
    c_ownv = din("c_ownv", [128, NJ * NB])
    out_own = nc.dram_tensor("out_own", [NJ * 128, D], F32, kind="ExternalOutput").ap()

    def dscr(name, shape, dt=BF16):
        return nc.dram_tensor(name, list(shape), dt, kind="Internal").ap(), Buf(name)

    KaT_s, KaT_b = dscr("KaT_s", [8, 128, S]); KnT_s, KnT_b = dscr("KnT_s", [8, 128, S])
    KrT_s, KrT_b = dscr("KrT_s", [64, S])
    Va_s, Va_b = dscr("Va_s", [8, 128, NT, 129]); Vb_s, Vb_b = dscr("Vb_s", [8, 128, NT, 129])
    QaT_s, QaT_b = dscr("QaT_s", [128, 8, NJ, 128]); BT_s, BT_b = dscr("BT_s", [64, 8, NJ, 128])
    QnT_s, QnT_b = dscr("QnT_s", [128, 8, NJ, 128]); QrT_s, QrT_b = dscr("QrT_s", [64, 8, NJ, 128])
    y_s, y_b = dscr("y_s", [2, 8, 128, NJ * 128])
    wg_h, wg_hb = dscr("wg_h", [D, 4096]); wa_h, wa_hb = dscr("wa_h", [1024, D]); wb_h, wb_hb = dscr("wb_h", [1024, D])
    wout_h, wout_hb = dscr("wout_h", [D, D]); wgate_h, wgate_hb = dscr("wgate_h", [D, DFF]); wup_h, wup_hb = dscr("wup_h", [D, DFF])
    wdown_h, wdown_hb = dscr("wdown_h", [DFF, D])
    precast = []
    if not small:
        for (dst, db, src, R, C) in ((wg_h, wg_hb, wg, D, 4096), (wa_h, wa_hb, wa, 1024, D), (wb_h, wb_hb, wb, 1024, D), (wout_h, wout_hb, wout, D, D),
                                     (wgate_h, wgate_hb, w_gate, D, DFF), (wup_h, wup_hb, w_up, D, DFF), (wdown_h, wdown_hb, w_down, DFF, D)):
            for r0 in range(0, R, 2048):
                r1 = min(R, r0 + 2048)
                for c0 in range(0, C, 2048):
                    c1 = min(C, c0 + 2048)
                    precast.append((dst[r0:r1, c0:c1], src[r0:r1, c0:c1], db))

    def emit_precast(n):
        for _ in range(n):
            if precast:
                o_, i_, b_ = precast.pop(0)
                sc.dma("pool", o_, i_, writes=[b_], chain=True)
    out_b = Buf("out")
    scr = {"KaT_s": (KaT_s, KaT_b), "KnT_s": (KnT_s, KnT_b), "KrT_s": (KrT_s, KrT_b), "Va_s": (Va_s, Va_b), "Vb_s": (Vb_s, Vb_b),
           "QaT_s": (QaT_s, QaT_b), "BT_s": (BT_s, BT_b), "QnT_s": (QnT_s, QnT_b), "QrT_s": (QrT_s, QrT_b), "y_s": (y_s, y_b)}

    def finish():
        for name in debug:
            if name not in scr:
                continue
            a, b = scr[name]
            o = nc.dram_tensor("dbg_" + name, list(a.shape), BF16, kind="ExternalOutput").ap()
            sc.dma("sp", o, a, reads=[b], writes=[Buf("dbg_" + name)])
        sc.finalize()
        return nc

    ps = []
    for i in range(6):
        ps.append((nc.alloc_psum_tensor("ps%d" % i, [128, 512], F32).ap(), Buf("ps%d" % i, excl=True)))
    pt = []
    for i in range(2):
        pt.append((nc.alloc_psum_tensor("pt%d" % i, [128, 1024], BF16).ap(), Buf("pt%d" % i, excl=True)))
    rr = {"ps": 0, "pt": 0}

    def next_ps():
        rr["ps"] = (rr["ps"] + 1) % 6
        return ps[rr["ps"]]

    def next_pt():
        rr["pt"] = (rr["pt"] + 1) % 2
        return pt[rr["pt"]]

    ident, ident_b = ar.alloc([128, 128], BF16, "ident")
    invA, invA_b = ar.alloc([128, 16], F32, "invA")
    invB, invB_b = ar.alloc([128, 32], F32, "invB")
    kmean, kmean_b = ar.alloc([128, 8, NB], F32, "kmean")
    kmean_bf, kmean_bf_b = ar.alloc([128, 8, NB], BF16, "kmeanbf")
    eps_t, eps_b = ar.alloc([128, 1], F32, "eps")
    sc.dma("sp", ident, c_ident, writes=[ident_b])
    sc.dma("sp", invA, c_invA, writes=[invA_b])
    sc.dma("sp", invB, c_invB, writes=[invB_b])
    sc.op("dve", lambda e: e.memset(kmean, 0.0), writes=[kmean_b])
    sc.op("dve", lambda e: e.memset(eps_t, EPS), writes=[eps_b])
    PERSIST = ar.off

    def load_w(dst, dst_b, src, kc, c0, c1, chain=True):
        for k in range(kc):
            for cc in range(c0, c1, 2048):
                ce = min(c1, cc + 2048)
                sc.dma("pool", dst[:, k, cc - c0:ce - c0], src[k * 128:(k + 1) * 128, cc:ce], writes=[dst_b], chain=chain)

    def load_h(dst, dst_b, src_h, src_hb, kc, c0, c1):
        sc.dma("pool", dst[:, 0:kc, :], src_h.rearrange("(k p) n -> p k n", p=128)[:, :, c0:c1], reads=[src_hb], writes=[dst_b])

    def load_bcast(dst, dst_b, src, n):
        sc.dma("sp", dst, src.partition_broadcast(128)[:, 0, :], writes=[dst_b])

    def rmsnorm(src, src_b, n, gain, gain_b, dst, dst_b, tmp):
        (ss, ss_b), (rs, rs_b) = tmp
        sc.op("act", lambda e: e.activation(out=dst, in_=src, func=AF.Square, accum_out=ss),
              reads=[src_b], writes=[dst_b, ss_b])
        sc.op("act", lambda e: e.activation(out=rs, in_=ss, func=AF.Sqrt, bias=eps_t, scale=1.0 / n),
              reads=[ss_b, eps_b], writes=[rs_b])
        sc.op("dve", lambda e: e.reciprocal(out=rs, in_=rs), reads=[rs_b], writes=[rs_b])
        sc.op("dve", lambda e: e.scalar_tensor_tensor(out=dst, in0=src, scalar=rs, in1=gain, op0=ALU.mult, op1=ALU.mult),
              reads=[src_b, rs_b, gain_b], writes=[dst_b])

    def transposes(src, src_b, nblk, width, dst_fn, dst_b, evac="dve"):
        b0 = 0
        while b0 < nblk:
            nb = min(8, nblk - b0)
            (p, p_b) = next_pt()

            def f(e, b0=b0, nb=nb, p=p):
                ins = None
                for i in range(nb):
                    ins = e.transpose(p[:width, i * 128:(i + 1) * 128], src[:, (b0 + i) * width:(b0 + i + 1) * width], ident)
                return ins
            sc.op("pe", f, reads=[src_b, ident_b], writes=[p_b])
            d = dst_fn(b0, nb)
            pv = p[:width, :nb * 128].rearrange("p (b t) -> p b t", t=128)
            if evac == "dve":
                sc.op("dve", lambda e, d=d, pv=pv: e.tensor_copy(out=d, in_=pv), reads=[p_b], writes=[dst_b])
            else:
                sc.op("act", lambda e, d=d, pv=pv: e.copy(out=d, in_=pv), reads=[p_b], writes=[dst_b])
            b0 += nb

    def linear(xT, xT_b, nk, W, W_b, c0, ncol, tok0=0):
        (p, p_b) = next_ps()

        def f(e):
            ins = None
            for k in range(nk):
                ins = e.matmul(p[:, :ncol], lhsT=xT[:, k, tok0:tok0 + 128], rhs=W[:, k, c0:c0 + ncol],
                               start=(k == 0), stop=(k == nk - 1))
            return ins
        sc.op("pe", f, reads=[xT_b, W_b], writes=[p_b])
        return p, p_b

    def rope_tables(posf_col, posf_b, inv, inv_b, n, tmp):
        (ang, ang_b), (u, u_b), (ki, ki_b), (kf, kf_b), (g, g_b), (cs, cs_b) = tmp
        sc.op("dve", lambda e: e.tensor_scalar(out=ang[:, :n], in0=inv, scalar1=posf_col, scalar2=None, op0=ALU.mult),
              reads=[posf_b, inv_b], writes=[ang_b])
        sc.op("dve", lambda e: e.tensor_scalar(out=u[:, 0, :n], in0=ang[:, :n], scalar1=TWO_PI_INV, scalar2=0.25, op0=ALU.mult, op1=ALU.add),
              reads=[ang_b], writes=[u_b])
        sc.op("dve", lambda e: e.tensor_scalar(out=u[:, 1, :n], in0=ang[:, :n], scalar1=TWO_PI_INV, scalar2=None, op0=ALU.mult),
              reads=[ang_b, u_b], writes=[u_b])
        sc.op("dve", lambda e: e.tensor_copy(out=ki[:, :, :n], in_=u[:, :, :n]), reads=[u_b], writes=[ki_b])
        sc.op("dve", lambda e: e.tensor_copy(out=kf[:, :, :n], in_=ki[:, :, :n]), reads=[ki_b], writes=[kf_b])
        sc.op("dve", lambda e: e.tensor_tensor(out=u[:, :, :n], in0=u[:, :, :n], in1=kf[:, :, :n], op=ALU.subtract),
              reads=[u_b, kf_b], writes=[u_b])
        sc.op("dve", lambda e: e.tensor_single_scalar(out=g[:, :, :n], in_=u[:, :, :n], scalar=0.5, op=ALU.is_gt), reads=[u_b], writes=[g_b])
        sc.op("dve", lambda e: e.tensor_tensor(out=u[:, :, :n], in0=u[:, :, :n], in1=g[:, :, :n], op=ALU.subtract), reads=[u_b, g_b], writes=[u_b])
        sc.op("dve", lambda e: e.tensor_single_scalar(out=g[:, :, :n], in_=u[:, :, :n], scalar=-0.5, op=ALU.is_lt), reads=[u_b], writes=[g_b])
        sc.op("dve", lambda e: e.tensor_tensor(out=u[:, :, :n], in0=u[:, :, :n], in1=g[:, :, :n], op=ALU.add), reads=[u_b, g_b], writes=[u_b])
        sc.op("act", lambda e: e.activation(out=cs[:, :, :n], in_=u[:, :, :n], func=AF.Sin, scale=SIN_SCALE), reads=[u_b], writes=[cs_b])
        return cs, cs_b

    def rope_apply(src3, src_b, dst3, dst_b, H, r0, n, cs, cs_b, tmp):
        (t1, t1_b), (t2, t2_b) = tmp
        cosb = cs[:, 0, :n].unsqueeze(1).to_broadcast([128, H, n])
        sinb = cs[:, 1, :n].unsqueeze(1).to_broadcast([128, H, n])
        x1 = src3[:, :, r0:r0 + n]; x2 = src3[:, :, r0 + n:r0 + 2 * n]
        a1 = t1[:, :H * n].rearrange("p (h n) -> p h n", n=n); a2 = t2[:, :H * n].rearrange("p (h n) -> p h n", n=n)
        sc.op("dve", lambda e: e.tensor_tensor(out=a1, in0=x1, in1=cosb, op=ALU.mult), reads=[src_b, cs_b], writes=[t1_b])
        sc.op("dve", lambda e: e.tensor_tensor(out=a2, in0=x2, in1=sinb, op=ALU.mult), reads=[src_b, cs_b], writes=[t2_b])
        sc.op("dve", lambda e: e.tensor_tensor(out=dst3[:, :, r0:r0 + n], in0=a1, in1=a2, op=ALU.subtract), reads=[t1_b, t2_b], writes=[dst_b])
        sc.op("dve", lambda e: e.tensor_tensor(out=a1, in0=x2, in1=cosb, op=ALU.mult), reads=[src_b, cs_b], writes=[t1_b])
        sc.op("dve", lambda e: e.tensor_tensor(out=a2, in0=x1, in1=sinb, op=ALU.mult), reads=[src_b, cs_b], writes=[t2_b])
        sc.op("dve", lambda e: e.tensor_tensor(out=dst3[:, :, r0 + n:r0 + 2 * n], in0=a1, in1=a2, op=ALU.add), reads=[t1_b, t2_b], writes=[dst_b])

    def norm_tmps():
        return (ar.alloc([128, 1], F32, "ss"), ar.alloc([128, 1], F32, "rs"))

    def rope_tmps():
        return (ar.alloc([128, 32], F32, "ang"), ar.alloc([128, 2, 32], F32, "u"), ar.alloc([128, 2, 32], I32, "ki"),
                ar.alloc([128, 2, 32], F32, "kf"), ar.alloc([128, 2, 32], F32, "g"))

    def x_to_xnT(xsrc, row0, xs, xn, xnT, gA, gA_b, ntmp, q="sp"):
        sc.dma(q, xs[0], xsrc[row0:row0 + 128, :], writes=[xs[1]])
        rmsnorm(xs[0], xs[1], D, gA, gA_b, xn[0], xn[1], ntmp)
        transposes(xn[0], xn[1], 16, 128, lambda b0, nb: xnT[0][:, b0:b0 + nb, :], xnT[1])

    ar.reset(PERSIST)
    Wk, Wk_b = ar.alloc([128, 16, 2368], BF16, "Wk")
    Wukv, Wukv_b = ar.alloc([128, 2, 2048], BF16, "Wukv")
    gA, gA_b = ar.alloc([128, D], F32, "gA")
    gKV, gKV_b = ar.alloc([128, 256], F32, "gKV")
    posi, posi_b = ar.alloc([128, NT], I32, "posi")
    posf, posf_b = ar.alloc([128, NT], F32, "posf")
    load_w(Wk, Wk_b, wk, 16, 0, 2368)
    load_w(Wukv, Wukv_b, w_ukv, 2, 0, 2048)
    load_bcast(gA, gA_b, g_attn, D)
    load_bcast(gKV, gKV_b, g_kv, 256)
    sc.dma("sp", posi, posT_all, writes=[posi_b])
    sc.op("dve", lambda e, a=posf, b=posi: e.tensor_copy(out=a, in_=b), reads=[posi_b], writes=[posf_b])
    ntmpA = [norm_tmps() for _ in range(2)]
    ntmpC = norm_tmps()
    rtA = rope_tmps() + (ar.alloc([128, 2, 32], F32, "csA"),)
    rtB = rope_tmps() + (ar.alloc([128, 2, 32], F32, "csB"),)
    rt12 = (ar.alloc([128, 256], F32, "t1"), ar.alloc([128, 256], F32, "t2"))
    xs2 = [ar.alloc([128, D], F32, "xs") for _ in range(2)]
    xn2 = [ar.alloc([128, D], BF16, "xn") for _ in range(2)]
    xnT2 = [ar.alloc([128, 16, 128], BF16, "xnT") for _ in range(2)]
    ka_sb2 = [ar.alloc([128, 1024], BF16, "ka") for _ in range(2)]
    kn_sb2 = [ar.alloc([128, 1024], BF16, "kn")] * 2
    ckvn2 = [ar.alloc([128, 256], BF16, "ckvn") for _ in range(2)]
    ckvnT2 = [ar.alloc([128, 2, 128], BF16, "ckvnT") for _ in range(2)]
    kr_sb2 = [ar.alloc([128, 64], BF16, "kr") for _ in range(2)]
    kaT_st2 = [ar.alloc([128, 8, 512], BF16, "kaTst") for _ in range(2)]
    knT_st2 = [ar.alloc([128, 8, 512], BF16, "knTst") for _ in range(2)]
    krT_st2 = [ar.alloc([64, 512], BF16, "krTst") for _ in range(2)]
    va_st2 = [ar.alloc([128, 4, 8, 129], BF16, "vast") for _ in range(2)]
    vb_st2 = [ar.alloc([128, 4, 8, 129], BF16, "vbst") for _ in range(2)]
    ksum = ar.alloc([128, 8], F32, "ksum")
    for q_ in range(2):
        sc.op("pool", lambda e, q_=q_: e.memset(va_st2[q_][0], 1.0), writes=[va_st2[q_][1]])
        sc.op("pool", lambda e, q_=q_: e.memset(vb_st2[q_][0], 1.0), writes=[vb_st2[q_][1]])
    if stop == 0:
        return finish()

    def stageA0(i):
        xs = xs2[i % 2]; xn = xn2[i % 2]
        sc.dma("pool", xs[0], x_all[i * 128:(i + 1) * 128, :], writes=[xs[1]])
        rmsnorm(xs[0], xs[1], D, gA, gA_b, xn[0], xn[1], ntmpA[i % 2])

    def stageA(i):
        xn = xn2[i % 2]; xnT = xnT2[i % 2]
        tl = i % 4; G = i // 4
        ka_sb = ka_sb2[i % 2]; kr_sb = kr_sb2[i % 2]; ckvn = ckvn2[i % 2]; va_st = va_st2[G % 2]
        transposes(xn[0], xn[1], 16, 128, lambda b0, nb: xnT[0][:, b0:b0 + nb, :], xnT[1])
        csA, csA_b = rope_tables(posf[:, i:i + 1], posf_b, invA, invA_b, 16, rtA)
        csB, csB_b = rope_tables(posf[:, i:i + 1], posf_b, invB, invB_b, 32, rtB)
        p, p_b = linear(xnT[0], xnT[1], 16, Wk, Wk_b, 2048, 320)
        rmsnorm(p[:, 0:256], p_b, 256, gKV, gKV_b, ckvn[0], ckvn[1], ntmpC)
        rope_apply(p[:, 256:320].rearrange("p (h d) -> p h d", d=64), p_b, kr_sb[0].rearrange("p (h d) -> p h d", d=64), kr_sb[1], 1, 0, 32, csB, csB_b, rt12)
        for hg in range(2):
            p, p_b = linear(xnT[0], xnT[1], 16, Wk, Wk_b, hg * 512, 512)
            dst = ka_sb[0][:, hg * 512:(hg + 1) * 512]
            sc.op("act", lambda e, dst=dst, p=p: e.copy(out=dst, in_=p), reads=[p_b], writes=[ka_sb[1]])
            rope_apply(p.rearrange("p (h d) -> p h d", d=128), p_b, dst.rearrange("p (h d) -> p h d", d=128), ka_sb[1], 4, 0, 16, csA, csA_b, rt12)
        for hg in range(2):
            p, p_b = linear(xnT[0], xnT[1], 16, Wk, Wk_b, 1024 + hg * 512, 512)
            dst = va_st[0][:, tl, hg * 4:(hg + 1) * 4, 0:128]
            sc.op("act", lambda e, dst=dst, p=p: e.copy(out=dst, in_=p.rearrange("p (h d) -> p h d", d=128)), reads=[p_b], writes=[va_st[1]])

    def stageB1(i):
        tl = i % 4; G = i // 4
        ka_sb = ka_sb2[i % 2]; kr_sb = kr_sb2[i % 2]; ckvn = ckvn2[i % 2]; ckvnT = ckvnT2[i % 2]; kn_sb = kn_sb2[i % 2]
        kaT_st = kaT_st2[G % 2]; krT_st = krT_st2[G % 2]; vb_st = vb_st2[G % 2]
        transposes(ka_sb[0], ka_sb[1], 8, 128, lambda b0, nb: kaT_st[0][:, b0:b0 + nb, tl * 128:(tl + 1) * 128], kaT_st[1])
        sc.op("dve", lambda e, tl=tl, kaT_st=kaT_st: e.tensor_reduce(out=ksum[0], in_=kaT_st[0][:, :, tl * 128:(tl + 1) * 128], axis=AX.X, op=ALU.add),
              reads=[kaT_st[1]], writes=[ksum[1]])
        nblk = i // 2
        sc.op("dve", lambda e, nblk=nblk: e.tensor_tensor(out=kmean[:, :, nblk], in0=kmean[:, :, nblk], in1=ksum[0], op=ALU.add),
              reads=[ksum[1], kmean_b], writes=[kmean_b])
        transposes(kr_sb[0], kr_sb[1], 1, 64, lambda b0, nb: krT_st[0][:, tl * 128:(tl + 1) * 128].unsqueeze(1), krT_st[1])
        transposes(ckvn[0], ckvn[1], 2, 128, lambda b0, nb: ckvnT[0][:, b0:b0 + nb, :], ckvnT[1])
        for hg in range(2):
            p, p_b = linear(ckvnT[0], ckvnT[1], 2, Wukv, Wukv_b, hg * 512, 512)
            dst = kn_sb[0][:, hg * 512:(hg + 1) * 512]
            sc.op("act", lambda e, dst=dst, p=p: e.copy(out=dst, in_=p), reads=[p_b], writes=[kn_sb[1]])
        for hg in range(2):
            p, p_b = linear(ckvnT[0], ckvnT[1], 2, Wukv, Wukv_b, 1024 + hg * 512, 512)
            dst = vb_st[0][:, tl, hg * 4:(hg + 1) * 4, 0:128]
            sc.op("act", lambda e, dst=dst, p=p: e.copy(out=dst, in_=p.rearrange("p (h d) -> p h d", d=128)), reads=[p_b], writes=[vb_st[1]])

    def stageB2(i):
        tl = i % 4; G = i // 4
        kn_sb = kn_sb2[i % 2]
        kaT_st = kaT_st2[G % 2]; krT_st = krT_st2[G % 2]; vb_st = vb_st2[G % 2]; knT_st = knT_st2[G % 2]; va_st = va_st2[G % 2]
        transposes(kn_sb[0], kn_sb[1], 8, 128, lambda b0, nb: knT_st[0][:, b0:b0 + nb, tl * 128:(tl + 1) * 128], knT_st[1])
        if tl == 3:
            c0 = G * 512
            sc.dma("sp", KaT_s[:, :, c0:c0 + 512].rearrange("h d t -> d h t"), kaT_st[0], reads=[kaT_st[1]], writes=[KaT_b])
            sc.dma("sp", KnT_s[:, :, c0:c0 + 512].rearrange("h d t -> d h t"), knT_st[0], reads=[knT_st[1]], writes=[KnT_b])
            sc.dma("sp", KrT_s[:, c0:c0 + 512], krT_st[0], reads=[krT_st[1]], writes=[KrT_b])
            for tt in range(4):
                sc.dma("sp", Va_s[:, :, 4 * G + tt, :].rearrange("h p d -> p h d"), va_st[0][:, tt, :, :], reads=[va_st[1]], writes=[Va_b])
                sc.dma("sp", Vb_s[:, :, 4 * G + tt, :].rearrange("h p d -> p h d"), vb_st[0][:, tt, :, :], reads=[vb_st[1]], writes=[Vb_b])

    NTA = 4 if stop == 1 else NT
    stageA0(0)
    if NTA > 1:
        stageA0(1)
    stageA(0)
    for i in range(NTA):
        stageB1(i)
        if i + 1 < NTA:
            stageA(i + 1)
        if i + 2 < NTA:
            stageA0(i + 2)
        if i % 4 == 1:
            emit_precast(1)
        stageB2(i)
    emit_precast(len(precast))
    sc.op("dve", lambda e: e.tensor_scalar(out=kmean_bf, in0=kmean, scalar1=1.0 / 256.0, scalar2=None, op0=ALU.mult),
          reads=[kmean_b], writes=[kmean_bf_b])
    sc.barrier()
    if stop in (1, 2):
        return finish()

    ar.reset(PERSIST)
    Wq, Wq_b = ar.alloc([128, 16, 1536], BF16, "Wq")
    Wuq, Wuq_b = ar.alloc([128, 4, 1536], BF16, "Wuq")
    gA, gA_b = ar.alloc([128, D], F32, "gA")
    gQ, gQ_b = ar.alloc([128, 512], F32, "gQ")
    posi, posi_b = ar.alloc([128, NJ], I32, "posi")
    posf, posf_b = ar.alloc([128, NJ], F32, "posf")
    pastv, pastv_b = ar.alloc([128, NJ, NB], F32, "pastv")
    pastneg, pastneg_b = ar.alloc([128, NJ, NB], F32, "pastneg")
    ownv, ownv_b = ar.alloc([128, NJ, NB], F32, "ownv")
    load_w(Wq, Wq_b, wq, 16, 0, 1536)
    load_w(Wuq, Wuq_b, w_uq, 4, 0, 1536)
    load_bcast(gA, gA_b, g_attn, D)
    load_bcast(gQ, gQ_b, g_q, 512)
    sc.dma("sp", posi, posT_own, writes=[posi_b])
    sc.op("dve", lambda e, a=posf, b=posi: e.tensor_copy(out=a, in_=b), reads=[posi_b], writes=[posf_b])
    sc.dma("sp", pastv, c_pastv.rearrange("p (j n) -> p j n", n=NB), writes=[pastv_b])
    sc.dma("sp", pastneg, c_pastneg.rearrange("p (j n) -> p j n", n=NB), writes=[pastneg_b])
    sc.dma("sp", ownv, c_ownv.rearrange("p (j n) -> p j n", n=NB), writes=[ownv_b])
    ntmp = norm_tmps()
    rtA = rope_tmps() + (ar.alloc([128, 2, 32], F32, "csA"),)
    rtB = rope_tmps() + (ar.alloc([128, 2, 32], F32, "csB"),)
    rt12 = (ar.alloc([128, 256], F32, "t1"), ar.alloc([128, 256], F32, "t2"))
    xs2 = [ar.alloc([128, D], F32, "xs") for _ in range(2)]
    xn2 = [ar.alloc([128, D], BF16, "xn") for _ in range(2)]
    xnT2 = [ar.alloc([128, 16, 128], BF16, "xnT") for _ in range(2)]
    qa_sb = ar.alloc([128, 1024], BF16, "qa")
    qaT_st = ar.alloc([128, 8, 128], BF16, "qaTst")
    cqn = ar.alloc([128, 512], BF16, "cqn")
    cqnT = ar.alloc([128, 4, 128], BF16, "cqnT")
    qn_sb = ar.alloc([128, 1024], BF16, "qn")
    qr_sb = ar.alloc([128, 512], BF16, "qr")
    qnT_st = ar.alloc([128, 8, 128], BF16, "qnTst")
    qrT_st = ar.alloc([64, 8, 128], BF16, "qrTst")
    gate = ar.alloc([128, 8, NB], F32, "gate")
    top8 = ar.alloc([128, 8, 8], F32, "top8")
    sel = ar.alloc([128, 8, NB], F32, "sel")
    bias_bf = ar.alloc([128, 8 * NB], BF16, "biasbf")
    bT_st = ar.alloc([64, 8, 128], BF16, "bTst")

    for j in range(NJ):
        xs = xs2[j % 2]; xn = xn2[j % 2]; xnT = xnT2[j % 2]
        x_to_xnT(x_own, j * 128, xs, xn, xnT, gA, gA_b, ntmp)
        csA, csA_b = rope_tables(posf[:, j:j + 1], posf_b, invA, invA_b, 16, rtA)
        csB, csB_b = rope_tables(posf[:, j:j + 1], posf_b, invB, invB_b, 32, rtB)
        for hg in range(2):
            p, p_b = linear(xnT[0], xnT[1], 16, Wq, Wq_b, hg * 512, 512)
            dst = qa_sb[0][:, hg * 512:(hg + 1) * 512]
            sc.op("act", lambda e, dst=dst, p=p: e.copy(out=dst, in_=p), reads=[p_b], writes=[qa_sb[1]])
            rope_apply(p.rearrange("p (h d) -> p h d", d=128), p_b, dst.rearrange("p (h d) -> p h d", d=128), qa_sb[1], 4, 0, 16, csA, csA_b, rt12)
        transposes(qa_sb[0], qa_sb[1], 8, 128, lambda b0, nb: qaT_st[0][:, b0:b0 + nb, :], qaT_st[1])
        sc.dma("sp", QaT_s[:, :, j, :], qaT_st[0], reads=[qaT_st[1]], writes=[QaT_b])
        (pg, pg_b) = next_ps()

        def fg(e, pg=pg):
            ins = None
            for h in range(8):
                ins = e.matmul(pg[:, h * NB:(h + 1) * NB], lhsT=qaT_st[0][:, h, :], rhs=kmean_bf[:, h, :], start=True, stop=True)
            return ins
        sc.op("pe", fg, reads=[qaT_st[1], kmean_bf_b], writes=[pg_b])
        pvb = pastv[:, j, :].unsqueeze(1).to_broadcast([128, 8, NB])
        pnb = pastneg[:, j, :].unsqueeze(1).to_broadcast([128, 8, NB])
        owb = ownv[:, j, :].unsqueeze(1).to_broadcast([128, 8, NB])
        pg3 = pg.rearrange("p (h n) -> p h n", n=NB)
        sc.op("dve", lambda e, pg3=pg3, pvb=pvb: e.tensor_tensor(out=gate[0], in0=pg3, in1=pvb, op=ALU.mult), reads=[pg_b, pastv_b], writes=[gate[1]])
        sc.op("dve", lambda e, pnb=pnb: e.tensor_tensor(out=gate[0], in0=gate[0], in1=pnb, op=ALU.add), reads=[gate[1], pastneg_b], writes=[gate[1]])
        if "gate" in debug and j == 1:
            og = nc.dram_tensor("dbg_gate", [128, 8, NB], F32, kind="ExternalOutput").ap()
            sc.dma("sp", og, gate[0], reads=[gate[1]], writes=[Buf("dbg_gate")])
            ok = nc.dram_tensor("dbg_kmean", [128, 8, NB], F32, kind="ExternalOutput").ap()
            sc.dma("sp", ok, kmean, reads=[kmean_b], writes=[Buf("dbg_kmean")])
        for h in range(8):
            sc.op("dve", lambda e, h=h: e.max(out=top8[0][:, h, :], in_=gate[0][:, h, :]), reads=[gate[1]], writes=[top8[1]])
        for h in range(8):
            sc.op("dve", lambda e, h=h: e.tensor_scalar(out=sel[0][:, h, :], in0=gate[0][:, h, :], scalar1=top8[0][:, h, 2:3], scalar2=None, op0=ALU.is_ge),
                  reads=[gate[1], top8[1]], writes=[sel[1]])
        sc.op("dve", lambda e, pvb=pvb: e.tensor_tensor(out=sel[0], in0=sel[0], in1=pvb, op=ALU.mult), reads=[sel[1], pastv_b], writes=[sel[1]])
        sc.op("dve", lambda e, owb=owb: e.tensor_tensor(out=sel[0], in0=sel[0], in1=owb, op=ALU.add), reads=[sel[1], ownv_b], writes=[sel[1]])
        sc.op("dve", lambda e: e.tensor_scalar(out=bias_bf[0], in0=sel[0].rearrange("p h n -> p (h n)"), scalar1=-1.0, scalar2=BIG, op0=ALU.add, op1=ALU.mult),
              reads=[sel[1]], writes=[bias_bf[1]])
        transposes(bias_bf[0], bias_bf[1], 8, NB, lambda b0, nb: bT_st[0][:, b0:b0 + nb, :], bT_st[1])
        sc.dma("sp", BT_s[:, :, j, :], bT_st[0], reads=[bT_st[1]], writes=[BT_b])
        p, p_b = linear(xnT[0], xnT[1], 16, Wq, Wq_b, 1024, 512)
        rmsnorm(p, p_b, 512, gQ, gQ_b, cqn[0], cqn[1], ntmp)
        transposes(cqn[0], cqn[1], 4, 128, lambda b0, nb: cqnT[0][:, b0:b0 + nb, :], cqnT[1])
        for hg in range(2):
            p, p_b = linear(cqnT[0], cqnT[1], 4, Wuq, Wuq_b, hg * 512, 512)
            dst = qn_sb[0][:, hg * 512:(hg + 1) * 512]
            sc.op("act", lambda e, dst=dst, p=p: e.copy(out=dst, in_=p), reads=[p_b], writes=[qn_sb[1]])
        p, p_b = linear(cqnT[0], cqnT[1], 4, Wuq, Wuq_b, 1024, 512)
        rope_apply(p.rearrange("p (h d) -> p h d", d=64), p_b, qr_sb[0].rearrange("p (h d) -> p h d", d=64), qr_sb[1], 8, 0, 32, csB, csB_b, rt12)
        transposes(qn_sb[0], qn_sb[1], 8, 128, lambda b0, nb: qnT_st[0][:, b0:b0 + nb, :], qnT_st[1])
        transposes(qr_sb[0], qr_sb[1], 8, 64, lambda b0, nb: qrT_st[0][:, b0:b0 + nb, :], qrT_st[1])
        sc.dma("sp", QnT_s[:, :, j, :], qnT_st[0], reads=[qnT_st[1]], writes=[QnT_b])
        sc.dma("sp", QrT_s[:, :, j, :], qrT_st[0], reads=[qrT_st[1]], writes=[QrT_b])
    sc.barrier()
    if stop == 3:
        return finish()

    ar.reset(PERSIST)
    K1 = [ar.alloc([128, S], BF16, "K1") for _ in range(2)]
    V = [ar.alloc([128, NT, 129], BF16, "V") for _ in range(2)]
    K2 = ar.alloc([128, max(S, 8192)], BF16, "K2")
    Q1 = [ar.alloc([128, NJ, 128], BF16, "Q1") for _ in range(2)]
    Q2 = [ar.alloc([128, NJ, 128], BF16, "Q2") for _ in range(2)]
    sc.op("pool", lambda e: e.memset(K2[0][64:128, :], 0.0), writes=[K2[1]])
    for q_ in range(2):
        sc.op("pool", lambda e, q_=q_: e.memset(Q2[q_][0][64:128, :, :], 0.0), writes=[Q2[q_][1]])
    dmask, dmask_b = ar.alloc([128, 8, 128], BF16, "dmask")
    PT = [ar.alloc([128, 4, 128], BF16, "PT") for _ in range(4)]
    Pacc = [ar.alloc([128, 512], F32, "Pacc") for _ in range(2)]
    rinv = ar.alloc([128, 512], F32, "rinv")
    ones_f, ones_f_b = ar.alloc([128, 128], F32, "onesf")
    sc.op("pool", lambda e: e.memset(ones_f, 1.0), writes=[ones_f_b])
    y_st = [ar.alloc([128, 512], BF16, "yst") for _ in range(2)]
    sc.dma("sp", dmask, c_dmask.rearrange("p (m q) -> p m q", q=128), writes=[dmask_b])
    hidx = 0
    qcnt = [0]
    gcnt = [0]

    def load_head(idx):
        br_, h_ = idx // 8, idx % 8
        K_s, K_sb = (KaT_s, KaT_b) if br_ == 0 else (KnT_s, KnT_b)
        V_s, V_sb = (Va_s, Va_b) if br_ == 0 else (Vb_s, Vb_b)
        Q1_s, Q1_sb = (QaT_s, QaT_b) if br_ == 0 else (QnT_s, QnT_b)
        Q2_s, Q2_sb = (BT_s, BT_b) if br_ == 0 else (QrT_s, QrT_b)
        k1 = K1[idx % 2]; v = V[idx % 2]; q1 = Q1[idx % 2]; q2 = Q2[idx % 2]
        sc.dma("sp", q1[0], Q1_s[:, h_, :, :], reads=[Q1_sb], writes=[q1[1]])
        sc.dma("sp", q2[0][0:64, :, :], Q2_s[:, h_, :, :], reads=[Q2_sb], writes=[q2[1]])
        for c0 in range(0, S, 4096):
            c1 = min(S, c0 + 4096)
            sc.dma("sp", k1[0][:, c0:c1], K_s[h_, :, c0:c1], reads=[K_sb], writes=[k1[1]], chain=True)
        for t0 in range(0, NT, 32):
            t1 = min(NT, t0 + 32)
            sc.dma("sp", v[0][:, t0:t1, :], V_s[h_, :, t0:t1, :], reads=[V_sb], writes=[v[1]], chain=True)

    load_head(0)
    for br in range(2):
        if br == 0:
            sc.dma("sp", K2[0][0:64, :64 * 128], c_E, writes=[K2[1]])
            scale = 128.0 ** -0.5
        else:
            sc.dma("sp", K2[0][0:64, :S], KrT_s, reads=[KrT_b], writes=[K2[1]])
            scale = 192.0 ** -0.5
        for h in range(8):
            k1 = K1[hidx % 2]; v = V[hidx % 2]; q1 = Q1[hidx % 2]; q2 = Q2[hidx % 2]
            hidx += 1
            if hidx < 16:
                load_head(hidx)
            for gq in range(NJ // 4):
                nkt = 32 * gq + 32
                gidx = gcnt[0]; gcnt[0] += 1
                (po, po_b) = ps[gidx % 2]
                pacc = Pacc[gidx % 2]
                ys = y_st[gidx % 2]
                pend = []

                def emit_pv(args):
                    kt, c0, ptile = args

                    def fv(e, kt=kt, c0=c0, ptile=ptile, v=v, po=po, nkt=nkt):
                        pf = ptile[0].rearrange("p a b -> p (a b)")
                        return e.matmul(po[:, c0:512], lhsT=v[0][:, kt, 0:128], rhs=pf[:, c0:512], start=(kt == 0), stop=(kt == nkt - 1))
                    sc.op("pe", fv, reads=[ptile[1], v[1]], writes=[po_b])

                for kt in range(nkt):
                    rel = kt - 32 * gq
                    jmin = 0 if rel < 8 else rel // 8
                    c0 = jmin * 128
                    (pq, pq_b) = ps[2 + (qcnt[0] % 4)]
                    ptile = PT[qcnt[0] % 4]
                    qcnt[0] += 1

                    def fs(e, pq=pq, kt=kt, k1=k1, q1=q1, q2=q2, gq=gq, br=br, jmin=jmin, c0=c0):
                        r1 = q1[0][:, 4 * gq + jmin:4 * gq + 4, :].rearrange("p j q -> p (j q)")
                        r2 = q2[0][:, 4 * gq + jmin:4 * gq + 4, :].rearrange("p j q -> p (j q)")
                        e.matmul(pq[:, c0:512], lhsT=k1[0][:, kt * 128:(kt + 1) * 128], rhs=r1, start=True, stop=False)
                        if br == 0:
                            l2 = K2[0][:, (kt // 2) * 128:(kt // 2 + 1) * 128]
                        else:
                            l2 = K2[0][:, kt * 128:(kt + 1) * 128]
                        return e.matmul(pq[:, c0:512], lhsT=l2, rhs=r2, start=False, stop=True)
                    sc.op("pe", fs, reads=[k1[1], q1[1], q2[1], K2[1]], writes=[pq_b])
                    pflat = ptile[0].rearrange("p a b -> p (a b)")
                    sc.op("act", lambda e, pq=pq, pflat=pflat, scale=scale, c0=c0: e.activation(out=pflat[:, c0:512], in_=pq[:, c0:512], func=AF.Exp, scale=scale),
                          reads=[pq_b], writes=[ptile[1]])
                    if rel >= 0:
                        m = rel - 8 * jmin
                        sc.op("dve", lambda e, ptile=ptile, m=m, jmin=jmin: e.tensor_tensor(out=ptile[0][:, jmin, :], in0=ptile[0][:, jmin, :], in1=dmask[:, m, :], op=ALU.mult),
                              reads=[ptile[1], dmask_b], writes=[ptile[1]])
                    if kt == 0:
                        sc.op("dve", lambda e, pacc=pacc, pflat=pflat: e.tensor_copy(out=pacc[0], in_=pflat), reads=[ptile[1]], writes=[pacc[1]])
                    else:
                        sc.op("dve", lambda e, pacc=pacc, pflat=pflat, c0=c0: e.tensor_tensor(out=pacc[0][:, c0:512], in0=pacc[0][:, c0:512], in1=pflat[:, c0:512], op=ALU.add),
                              reads=[ptile[1], pacc[1]], writes=[pacc[1]])
                    if len(pend) == 2:
                        emit_pv(pend.pop(0))
                    pend.append((kt, c0, ptile))
                while pend:
                    emit_pv(pend.pop(0))
                (pr, pr_b) = ps[2 + (qcnt[0] % 4)]
                qcnt[0] += 1
                sc.op("pe", lambda e, pr=pr, pacc=pacc: e.matmul(pr, lhsT=ones_f, rhs=pacc[0], start=True, stop=True), reads=[pacc[1], ones_f_b], writes=[pr_b])
                sc.op("dve", lambda e, pr=pr: e.reciprocal(out=rinv[0], in_=pr), reads=[pr_b], writes=[rinv[1]])
                sc.op("dve", lambda e, po=po, ys=ys: e.tensor_tensor(out=ys[0], in0=po, in1=rinv[0], op=ALU.mult), reads=[po_b, rinv[1]], writes=[ys[1]])
                sc.dma("sp", y_s[br, h, :, gq * 512:(gq + 1) * 512], ys[0], reads=[ys[1]], writes=[y_b])
    sc.barrier()
    if stop == 4:
        return finish()

    ar.reset(PERSIST)
    if os.environ.get("XPAD", "0") == "1":
        ar.alloc([128, 2048], BF16, "pad")
    ntmp = norm_tmps()
    h1 = [ar.alloc([128, D], F32, "h1") for _ in range(4)]
    xn = ar.alloc([128, D], BF16, "xn")
    xnT = ar.alloc([128, 16, 512], BF16, "xnT")
    Wc = [ar.alloc([128, 16, 512], BF16, "Wc") for _ in range(3)]
    CBASE = ar.off
    gA, gA_b = ar.alloc([128, D], F32, "gA")
    yin = [ar.alloc([128, 1024], BF16, "yin") for _ in range(2)]
    yT = [ar.alloc([128, 8, 512], BF16, "yT") for _ in range(2)]
    gates = ar.alloc([128, 4, 4096], BF16, "gates")
    mixed = ar.alloc([128, 4, D], BF16, "mixed")
    mixedT = ar.alloc([128, 16, 512], BF16, "mixedT")
    tmpf = ar.alloc([128, 512], F32, "tmpf")
    ar.reset(CBASE)
    gF, gF_b = ar.alloc([128, D], F32, "gF")
    gO, gO_b = ar.alloc([128, D], F32, "gO")
    actc = [ar.alloc([128, 512], BF16, "actc") for _ in range(2)]
    actT = ar.alloc([128, 44, 512], BF16, "actT")
    sg = ar.alloc([128, 512], F32, "sg")
    Wd = [ar.alloc([128, 11, 512], BF16, "Wd") for _ in range(2)]
    wci = [0]

    def next_w():
        wci[0] = (wci[0] + 1) % 3
        return Wc[wci[0]]

    for G in range(NG):
        load_bcast(gA, gA_b, g_attn, D)
        for t in range(4):
            row0 = (G * 4 + t) * 128
            sc.dma("sp", h1[t][0], x_own[row0:row0 + 128, :], writes=[h1[t][1]])
            rmsnorm(h1[t][0], h1[t][1], D, gA, gA_b, xn[0], xn[1], ntmp)
            transposes(xn[0], xn[1], 16, 128, lambda b0, nb, t=t: xnT[0][:, b0:b0 + nb, t * 128:(t + 1) * 128], xnT[1])
        for br in range(2):
            sc.dma("sp", yT[br][0], y_s[br, :, :, G * 512:(G + 1) * 512].rearrange("h d q -> d h q"), reads=[y_b], writes=[yT[br][1]])
        for cg in range(8):
            w = next_w()
            load_h(w[0], w[1], wg_h, wg_hb, 16, cg * 512, (cg + 1) * 512)
            for t in range(4):
                p, p_b = linear(xnT[0], xnT[1], 16, w[0], w[1], 0, 512, tok0=t * 128)
                sc.op("act", lambda e, p=p, t=t, cg=cg: e.activation(out=gates[0][:, t, cg * 512:(cg + 1) * 512], in_=p, func=AF.Sigmoid),
                      reads=[p_b], writes=[gates[1]])
        for cg in range(4):
            wA = next_w()
            load_h(wA[0][:, 0:8, :], wA[1], wa_h, wa_hb, 8, cg * 512, (cg + 1) * 512)
            load_h(wA[0][:, 8:16, :], wA[1], wb_h, wb_hb, 8, cg * 512, (cg + 1) * 512)
            for t in range(4):
                pa, pa_b = linear(yT[0][0], yT[0][1], 8, wA[0], wA[1], 0, 512, tok0=t * 128)
                (pb, pb_b) = next_ps()

                def fb(e, pb=pb, wA=wA, t=t):
                    ins = None
                    for k in range(8):
                        ins = e.matmul(pb, lhsT=yT[1][0][:, k, t * 128:(t + 1) * 128], rhs=wA[0][:, 8 + k, :], start=(k == 0), stop=(k == 7))
                    return ins
                sc.op("pe", fb, reads=[yT[1][1], wA[1]], writes=[pb_b])
                sc.op("dve", lambda e, pa=pa, t=t, cg=cg: e.tensor_tensor(out=tmpf[0], in0=pa, in1=gates[0][:, t, cg * 512:(cg + 1) * 512], op=ALU.mult),
                      reads=[pa_b, gates[1]], writes=[tmpf[1]])
                sc.op("dve", lambda e, pb=pb, t=t, cg=cg: e.tensor_tensor(out=pb, in0=pb, in1=gates[0][:, t, 2048 + cg * 512:2048 + (cg + 1) * 512], op=ALU.mult),
                      reads=[pb_b, gates[1]], writes=[pb_b])
                sc.op("dve", lambda e, pb=pb, t=t, cg=cg: e.tensor_tensor(out=mixed[0][:, t, cg * 512:(cg + 1) * 512], in0=pb, in1=tmpf[0], op=ALU.add),
                      reads=[pb_b, tmpf[1]], writes=[mixed[1]])
        for t in range(4):
            transposes(mixed[0][:, t, :], mixed[1], 16, 128, lambda b0, nb, t=t: mixedT[0][:, b0:b0 + nb, t * 128:(t + 1) * 128], mixedT[1])
        for cg in range(4):
            w = next_w()
            load_h(w[0], w[1], wout_h, wout_hb, 16, cg * 512, (cg + 1) * 512)
            for t in range(4):
                p, p_b = linear(mixedT[0], mixedT[1], 16, w[0], w[1], 0, 512, tok0=t * 128)
                sc.op("dve", lambda e, p=p, t=t, cg=cg: e.tensor_tensor(out=h1[t][0][:, cg * 512:(cg + 1) * 512], in0=p, in1=h1[t][0][:, cg * 512:(cg + 1) * 512], op=ALU.add),
                      reads=[p_b, h1[t][1]], writes=[h1[t][1]])
        sc.barrier()
        if stop == 5:
            return finish()
        load_bcast(gF, gF_b, g_ffn, D); load_bcast(gO, gO_b, g_fin, D)
        hnT = xnT
        for t in range(4):
            rmsnorm(h1[t][0], h1[t][1], D, gF, gF_b, xn[0], xn[1], ntmp)
            transposes(xn[0], xn[1], 16, 128, lambda b0, nb, t=t: hnT[0][:, b0:b0 + nb, t * 128:(t + 1) * 128], hnT[1])
        ai = 0
        for cg in range(11):
            wG = next_w(); wU = next_w()
            load_h(wG[0], wG[1], wgate_h, wgate_hb, 16, cg * 512, (cg + 1) * 512)
            load_h(wU[0], wU[1], wup_h, wup_hb, 16, cg * 512, (cg + 1) * 512)
            for t in range(4):
                pgt, pgt_b = linear(hnT[0], hnT[1], 16, wG[0], wG[1], 0, 512, tok0=t * 128)
                put, put_b = linear(hnT[0], hnT[1], 16, wU[0], wU[1], 0, 512, tok0=t * 128)
                ac = actc[ai % 2]; ai += 1
                sc.op("act", lambda e, pgt=pgt: e.activation(out=sg[0], in_=pgt, func=AF.Silu), reads=[pgt_b], writes=[sg[1]])
                sc.op("dve", lambda e, put=put, ac=ac: e.tensor_tensor(out=ac[0], in0=put, in1=sg[0], op=ALU.mult),
                      reads=[put_b, sg[1]], writes=[ac[1]])
                transposes(ac[0], ac[1], 4, 128, lambda b0, nb, t=t, cg=cg: actT[0][:, cg * 4 + b0:cg * 4 + b0 + nb, t * 128:(t + 1) * 128], actT[1], evac="act")
        if stop == 6:
            return finish()
        for cg in range(4):
            accs = [next_ps() for _ in range(4)]
            for fq in range(4):
                wd = Wd[(cg * 4 + fq) % 2]
                load_h(wd[0], wd[1], wdown_h[fq * 1408:(fq + 1) * 1408, :], wdown_hb, 11, cg * 512, (cg + 1) * 512)
                for t in range(4):
                    def fd(e, acc=accs[t][0], wd=wd, t=t, fq=fq):
                        ins = None
                        for f in range(11):
                            fc = fq * 11 + f
                            ins = e.matmul(acc, lhsT=actT[0][:, fc, t * 128:(t + 1) * 128], rhs=wd[0][:, f, :], start=(fc == 0), stop=(fc == 43))
                        return ins
                    sc.op("pe", fd, reads=[actT[1], wd[1]], writes=[accs[t][1]])
            XPC = int(os.environ.get("XPC", "0"))
            for t in range(4):
                if XPC == 1:
                    break
                sc.op("dve", lambda e, t=t, cg=cg, acc=accs[t][0]: e.tensor_tensor(out=h1[t][0][:, cg * 512:(cg + 1) * 512], in0=acc, in1=h1[t][0][:, cg * 512:(cg + 1) * 512], op=ALU.add),
                      reads=[accs[t][1], h1[t][1]], writes=[h1[t][1]])
            if XPC == 2:
                break
        if stop == 7:
            return finish()
        for t in range(4):
            row0 = (G * 4 + t) * 128
            (ss, ss_b), (rs, rs_b) = ntmp
            sc.op("act", lambda e, t=t: e.activation(out=xn[0], in_=h1[t][0], func=AF.Square, accum_out=ss), reads=[h1[t][1]], writes=[xn[1], ss_b])
            sc.op("act", lambda e: e.activation(out=rs, in_=ss, func=AF.Sqrt, bias=eps_t, scale=1.0 / D), reads=[ss_b, eps_b], writes=[rs_b])
            sc.op("dve", lambda e: e.reciprocal(out=rs, in_=rs), reads=[rs_b], writes=[rs_b])
            sc.op("dve", lambda e, t=t: e.scalar_tensor_tensor(out=h1[t][0], in0=h1[t][0], scalar=rs, in1=gO, op0=ALU.mult, op1=ALU.mult),
                  reads=[h1[t][1], rs_b, gO_b], writes=[h1[t][1]])
            sc.dma("sp", out_own[row0:row0 + 128, :], h1[t][0], reads=[h1[t][1]], writes=[out_b])
        sc.barrier()

    return finish()


def host_inputs(S, x, positions, attn_norm, w_in, q_norm, w_uq, kv_norm, w_ukv, w_branch_a, w_branch_b,
                w_out, ffn_norm, w_gate, w_up, w_down, final_norm):
    NT = S // 128
    NJ = NT // NCORE
    NB = 64
    f = lambda a: np.ascontiguousarray(np.asarray(a, dtype=np.float32))
    x2 = f(x).reshape(S, D)
    pos = np.asarray(positions).reshape(S).astype(np.int32)
    w_in0 = f(w_in)[0]
    common = {
        "x_all": x2,
        "posT_all": np.ascontiguousarray(pos.reshape(NT, 128).T),
        "wk": np.ascontiguousarray(np.concatenate([w_in0[:, 1024:2048], w_in0[:, 2048:3072], w_in0[:, 3584:3840], w_in0[:, 3840:3904]], axis=1)),
        "wq": np.ascontiguousarray(np.concatenate([w_in0[:, 0:1024], w_in0[:, 3072:3584]], axis=1)),
        "wg": np.ascontiguousarray(w_in0[:, 3904:8000]),
        "wa": f(w_branch_a)[0], "wb": f(w_branch_b)[0], "wout": f(w_out)[0],
        "w_gate": f(w_gate)[0], "w_up": f(w_up)[0], "w_down": f(w_down)[0],
        "g_attn": f(attn_norm).reshape(1, D), "g_q": f(q_norm).reshape(1, 512), "g_kv": f(kv_norm).reshape(1, 256),
        "g_ffn": f(ffn_norm).reshape(1, D), "g_fin": f(final_norm).reshape(1, D),
    }
    uq = f(w_uq)[0].reshape(512, 8, 192)
    common["w_uq"] = np.ascontiguousarray(np.concatenate([uq[:, :, :128].reshape(512, 1024), uq[:, :, 128:].reshape(512, 512)], axis=1))
    ukv = f(w_ukv)[0].reshape(256, 8, 256)
    common["w_ukv"] = np.ascontiguousarray(np.concatenate([ukv[:, :, :128].reshape(256, 1024), ukv[:, :, 128:].reshape(256, 1024)], axis=1))
    invA = (THETA ** (-np.arange(0, 32, 2, dtype=np.float32) / np.float32(32))).astype(np.float32)
    invB = (THETA ** (-np.arange(0, 64, 2, dtype=np.float32) / np.float32(64))).astype(np.float32)
    common["c_invA"] = np.ascontiguousarray(np.broadcast_to(invA, (128, 16)))
    common["c_invB"] = np.ascontiguousarray(np.broadcast_to(invB, (128, 32)))
    common["c_ident"] = np.eye(128, dtype=np.float32).astype(ml_dtypes.bfloat16)
    E = np.zeros((64, 64, 128), np.float32)
    for n in range(64):
        E[n, n, :] = 1.0
    common["c_E"] = E.reshape(64, 64 * 128).astype(ml_dtypes.bfloat16)
    maps = []
    kk = np.arange(128)[:, None]
    qq = np.arange(128)[None, :]
    for c in range(NCORE):
        m = dict(common)
        tiles = [8 * j + c for j in range(NJ)]
        m["x_own"] = np.ascontiguousarray(np.concatenate([x2[g * 128:(g + 1) * 128] for g in tiles], axis=0))
        m["posT_own"] = np.ascontiguousarray(np.stack([pos[g * 128:(g + 1) * 128] for g in tiles], axis=1))
        dm = np.zeros((128, 8, 128), np.float32)
        for mm in range(8):
            if mm < c:
                dm[:, mm, :] = 1.0
            elif mm == c:
                dm[:, mm, :] = (kk <= qq).astype(np.float32)
        m["c_dmask"] = dm.reshape(128, 8 * 128).astype(ml_dtypes.bfloat16)
        pv = np.zeros((NJ, NB), np.float32); ov = np.zeros((NJ, NB), np.float32)
        for j, g in enumerate(tiles):
            b = g // 2
            pv[j, :b] = 1.0
            ov[j, b] = 1.0
        pn = (pv - 1.0) * BIG
        m["c_pastv"] = np.ascontiguousarray(np.broadcast_to(pv.reshape(1, -1), (128, NJ * NB)))
        m["c_pastneg"] = np.ascontiguousarray(np.broadcast_to(pn.reshape(1, -1), (128, NJ * NB)))
        m["c_ownv"] = np.ascontiguousarray(np.broadcast_to(ov.reshape(1, -1), (128, NJ * NB)))
        maps.append(m)
    return maps


_CACHE = {}


def run(S, **inputs):
    if S not in _CACHE:
        _CACHE[S] = build_program(S)
    nc = _CACHE[S]
    maps = host_inputs(S, **inputs)
    import os
    ncr = int(os.environ.get('KDEBUG_CORES', NCORE))
    res = run_bass_kernel_spmd(nc, maps[:ncr], core_ids=list(range(ncr)))
    NT = S // 128
    NJ = NT // NCORE
    out = np.zeros((S, D), np.float32)
    for c in range(ncr):
        o = res.results[c]["out_own"]
        for j in range(NJ):
            g = 8 * j + c
            out[g * 128:(g + 1) * 128] = o[j * 128:(j + 1) * 128]
    return out.reshape(1, S, D)


def kernel(**inputs):
    S = int(np.asarray(inputs["x"]).shape[1])
    return run(S, **inputs)
```
